# Optimizing a Trainium2 kernel written in Bass

```python
import jax, jax.numpy as jnp
from jax import lax
import numpy as np

D_MODEL = 1024
BATCH = 4
SEQ = 4096
DEPTH = 2
DEC_BATCH = 128
DEC_SEQ = 4
PAST_LEN = 8192
PAGE_SIZE = 128

N_A_LAYERS = DEPTH // 2
N_B_LAYERS = DEPTH - N_A_LAYERS
ML_HEADS = 4
ML_DV = D_MODEL // ML_HEADS
ML_DK = ML_DV // 2
ML_CHUNK = 64
GATE_SOFTCAP = 15.0
ML_Q_END = ML_HEADS * ML_DK
ML_K_END = 2 * ML_HEADS * ML_DK
ML_V_END = ML_K_END + ML_HEADS * ML_DV
ML_O_END = ML_V_END + ML_HEADS * ML_DV
ML_I_END = ML_O_END + ML_HEADS
ML_IN = ML_I_END + ML_HEADS
ATT_HD = 64
ATT_QH = D_MODEL // ATT_HD
ATT_KVH = 4
ATT_GROUP = ATT_QH // ATT_KVH
WINDOW = 128
ATT_BLOCK = 128
D_FF = 4 * D_MODEL
EPS = 1e-6

kernel_name = "yoco_mlstm_swa_sink_decoder_step"


def rmsnorm(x, g):
    xf = x.astype(jnp.float32)
    y = xf * lax.rsqrt(jnp.mean(xf * xf, axis=-1, keepdims=True) + EPS)
    return (y * g.astype(jnp.float32)).astype(x.dtype)


def softcap(x):
    return GATE_SOFTCAP * jnp.tanh(x / GATE_SOFTCAP)


def mlstm_chunkwise(q, k, v, log_i, log_f, C0, n0, m0, chunk):
    B, L, H, DK = q.shape
    DV = v.shape[-1]
    nc = L // chunk

    def split(a):
        return jnp.moveaxis(a.reshape((B, nc, chunk) + a.shape[2:]), 1, 0)

    xs = (split(q), split(k), split(v), split(log_i), split(log_f))
    causal = jnp.tril(jnp.ones((chunk, chunk), dtype=bool))[None, :, :, None]

    def step(carry, inp):
        C, n, m = carry
        qc, kc, vc, lic, lfc = inp
        b = jnp.cumsum(lfc, axis=1)
        dmat = b[:, :, None, :] - b[:, None, :, :] + lic[:, None, :, :]
        dmat = jnp.where(causal, dmat, -jnp.inf)
        inter = b + m[:, None, :]
        m_t = jnp.maximum(inter, jnp.max(dmat, axis=2))
        w = jnp.exp(dmat - m_t[:, :, None, :])
        a_inter = jnp.exp(inter - m_t)
        s = jnp.einsum('bthd,bshd->btsh', qc, kc) * w
        num = jnp.einsum('btsh,bshv->bthv', s, vc) + a_inter[..., None] * jnp.einsum('bthd,bhdv->bthv', qc, C)
        den = jnp.sum(s, axis=2) + a_inter * jnp.einsum('bthd,bhd->bth', qc, n)
        h = num / jnp.maximum(jnp.abs(den), jnp.exp(-m_t))[..., None]
        m_new = m_t[:, -1]
        b_last = b[:, -1]
        decay = jnp.exp(b_last + m - m_new)
        wk = jnp.exp(b_last[:, None, :] - b + lic - m_new[:, None, :])
        C_new = decay[..., None, None] * C + jnp.einsum('bsh,bshd,bshv->bhdv', wk, kc, vc)
        n_new = decay[..., None] * n + jnp.einsum('bsh,bshd->bhd', wk, kc)
        return (C_new, n_new, m_new), h

    (C, n, m), hs = lax.scan(step, (C0, n0, m0), xs)
    h = jnp.moveaxis(hs, 0, 1).reshape(B, L, H, DV)
    return h, C, n, m


def mlstm_mixer(x, norm_g, w_in, b_i, b_f, head_g, w_out, C0, n0, m0, chunk):
    B, L, _ = x.shape
    p = rmsnorm(x, norm_g) @ w_in
    q = p[..., :ML_Q_END].reshape(B, L, ML_HEADS, ML_DK).astype(jnp.float32)
    k = p[..., ML_Q_END:ML_K_END].reshape(B, L, ML_HEADS, ML_DK).astype(jnp.float32) * (ML_DK ** -0.5)
    v = p[..., ML_K_END:ML_V_END].reshape(B, L, ML_HEADS, ML_DV).astype(jnp.float32)
    o = jax.nn.sigmoid(p[..., ML_V_END:ML_O_END].astype(jnp.float32))
    log_i = softcap(p[..., ML_O_END:ML_I_END].astype(jnp.float32) + b_i.astype(jnp.float32))
    log_f = jax.nn.log_sigmoid(softcap(p[..., ML_I_END:].astype(jnp.float32) + b_f.astype(jnp.float32)))
    h, C, n, m = mlstm_chunkwise(q, k, v, log_i, log_f, C0.astype(jnp.float32),
                                 n0.astype(jnp.float32), m0.astype(jnp.float32), chunk)
    h = rmsnorm(h, head_g.reshape(ML_HEADS, ML_DV)).reshape(B, L, ML_HEADS * ML_DV)
    return x + (o * h).astype(x.dtype) @ w_out, C, n, m


def sqrelu_mlp(x, g, w1, w2):
    return x + jnp.square(jax.nn.relu(rmsnorm(x, g) @ w1)) @ w2


def shared_kv(x, g, w_kv, k_norm_g):
    B, L, _ = x.shape
    kv = rmsnorm(x, g) @ w_kv
    k = kv[..., :ATT_KVH * ATT_HD].reshape(B, L, ATT_KVH, ATT_HD)
    v = kv[..., ATT_KVH * ATT_HD:].reshape(B, L, ATT_KVH, ATT_HD)
    return rmsnorm(k, k_norm_g), v


def sink_attention(q, k, v, mask, sinks):
    s = jnp.einsum('...qhgd,...khd->...hgqk', q.astype(jnp.float32), k.astype(jnp.float32)) * (ATT_HD ** -0.5)
    s = jnp.where(mask, s, -jnp.inf)
    sink_col = jnp.broadcast_to(sinks.astype(jnp.float32)[:, :, None, None], s.shape[:-1] + (1,))
    p = jax.nn.softmax(jnp.concatenate([s, sink_col], axis=-1), axis=-1)[..., :-1]
    return jnp.einsum('...hgqk,...khd->...qhgd', p, v.astype(jnp.float32))


def banded_window_attention(q, k, v, sinks):
    B, L = q.shape[:2]
    nb = L // ATT_BLOCK
    qb = q.reshape(B, nb, ATT_BLOCK, ATT_KVH, ATT_GROUP, ATT_HD)

    def band(a):
        ap = jnp.concatenate([jnp.zeros_like(a[:, :ATT_BLOCK]), a], axis=1)
        ap = ap.reshape(B, nb + 1, ATT_BLOCK, ATT_KVH, ATT_HD)
        return jnp.concatenate([ap[:, :-1], ap[:, 1:]], axis=2)

    kb, vb = band(k), band(v)
    qi = jnp.arange(ATT_BLOCK)[:, None] + ATT_BLOCK
    ki = jnp.arange(2 * ATT_BLOCK)[None, :]
    rel = qi - ki
    blk = jnp.arange(nb)[:, None, None]
    mask = (rel >= 0) & (rel <= WINDOW) & (blk * ATT_BLOCK - ATT_BLOCK + ki >= 0)
    o = sink_attention(qb, kb, vb, mask[None, :, None, None], sinks)
    return o.reshape(B, L, ATT_KVH, ATT_GROUP, ATT_HD)


def cached_window_attention(q, k, v, win_k, win_v, sinks):
    L = q.shape[1]
    W = win_k.shape[1]
    kk = jnp.concatenate([win_k.astype(k.dtype), k], axis=1)
    vv = jnp.concatenate([win_v.astype(v.dtype), v], axis=1)
    rel = (jnp.arange(L)[:, None] + W) - jnp.arange(W + L)[None, :]
    mask = (rel >= 0) & (rel <= WINDOW)
    o = sink_attention(q, kk, vv, mask, sinks)
    return o, kk[:, -W:], vv[:, -W:]


def swa_mixer(x, k, v, win_k, win_v, norm_g, w_q, q_norm_g, sinks, w_o):
    B, L, _ = x.shape
    q = (rmsnorm(x, norm_g) @ w_q).reshape(B, L, ATT_KVH, ATT_GROUP, ATT_HD)
    q = rmsnorm(q, q_norm_g)
    sk = sinks.reshape(ATT_KVH, ATT_GROUP)
    if win_k is None:
        o = banded_window_attention(q, k, v, sk)
        new_k, new_v = k[:, -WINDOW:], v[:, -WINDOW:]
    else:
        o, new_k, new_v = cached_window_attention(q, k, v, win_k, win_v, sk)
    return x + o.reshape(B, L, ATT_QH * ATT_HD).astype(x.dtype) @ w_o, new_k, new_v


def run_trunk(x, C0, n0, m0, win_k, win_v, chunk,
              ml_norm_g, ml_w_in, ml_b_i, ml_b_f, ml_head_g, ml_w_out,
              kv_norm_g, w_kv, k_norm_g,
              att_norm_g, att_w_q, q_norm_g, att_sinks, att_w_o,
              mlp_norm_g, mlp_w1, mlp_w2):
    Cs, ns, ms = [], [], []
    kv_k = kv_v = new_k = new_v = None
    for layer in range(DEPTH):
        if layer < N_A_LAYERS:
            x, C, n, m = mlstm_mixer(x, ml_norm_g[layer], ml_w_in[layer], ml_b_i[layer], ml_b_f[layer],
                                     ml_head_g[layer], ml_w_out[layer], C0[layer], n0[layer], m0[layer], chunk)
            Cs.append(C); ns.append(n); ms.append(m)
        else:
            j = layer - N_A_LAYERS
            x, new_k, new_v = swa_mixer(x, kv_k, kv_v, win_k, win_v, att_norm_g[j], att_w_q[j],
                                        q_norm_g[j], att_sinks[j], att_w_o[j])
        x = sqrelu_mlp(x, mlp_norm_g[layer], mlp_w1[layer], mlp_w2[layer])
        if layer == N_A_LAYERS - 1:
            kv_k, kv_v = shared_kv(x, kv_norm_g, w_kv, k_norm_g)
    return x, jnp.stack(Cs), jnp.stack(ns), jnp.stack(ms), new_k, new_v


def setup_inputs(seed: int = 0) -> dict:
    key = jax.random.key(seed)
    ks = jax.random.split(key, 32)
    f32 = jnp.float32
    nrm = lambda k, shape, s=1.0: (jax.random.normal(k, shape, f32) * s)
    return {
        "x_prompt": nrm(ks[0], (BATCH, SEQ, D_MODEL)),
        "x_sample": nrm(ks[1], (DEC_BATCH, DEC_SEQ, D_MODEL)),
        "state_mlstm_C": nrm(ks[2], (N_A_LAYERS, DEC_BATCH, ML_HEADS, ML_DK, ML_DV), 0.5),
        "state_mlstm_n": nrm(ks[3], (N_A_LAYERS, DEC_BATCH, ML_HEADS, ML_DK), 0.5),
        "state_mlstm_m": nrm(ks[4], (N_A_LAYERS, DEC_BATCH, ML_HEADS)),
        "cache_win_k": nrm(ks[5], (DEC_BATCH, WINDOW, ATT_KVH, ATT_HD)),
        "cache_win_v": nrm(ks[6], (DEC_BATCH, WINDOW, ATT_KVH, ATT_HD)),
        "ml_norm_g": 1.0 + nrm(ks[7], (N_A_LAYERS, D_MODEL), 0.02),
        "ml_w_in": nrm(ks[8], (N_A_LAYERS, D_MODEL, ML_IN), D_MODEL ** -0.5),
        "ml_b_i": nrm(ks[9], (N_A_LAYERS, ML_HEADS), 0.1),
        "ml_b_f": 3.0 + nrm(ks[10], (N_A_LAYERS, ML_HEADS), 0.1),
        "ml_head_g": 1.0 + nrm(ks[11], (N_A_LAYERS, ML_HEADS * ML_DV), 0.02),
        "ml_w_out": nrm(ks[12], (N_A_LAYERS, ML_HEADS * ML_DV, D_MODEL), (ML_HEADS * ML_DV) ** -0.5),
        "kv_norm_g": 1.0 + nrm(ks[13], (D_MODEL,), 0.02),
        "w_kv": nrm(ks[14], (D_MODEL, 2 * ATT_KVH * ATT_HD), D_MODEL ** -0.5),
        "k_norm_g": 1.0 + nrm(ks[15], (ATT_HD,), 0.02),
        "att_norm_g": 1.0 + nrm(ks[16], (N_B_LAYERS, D_MODEL), 0.02),
        "att_w_q": nrm(ks[17], (N_B_LAYERS, D_MODEL, ATT_QH * ATT_HD), D_MODEL ** -0.5),
        "q_norm_g": 1.0 + nrm(ks[18], (N_B_LAYERS, ATT_HD), 0.02),
        "att_sinks": nrm(ks[19], (N_B_LAYERS, ATT_QH), 0.5),
        "att_w_o": nrm(ks[20], (N_B_LAYERS, ATT_QH * ATT_HD, D_MODEL), (ATT_QH * ATT_HD) ** -0.5),
        "mlp_norm_g": 1.0 + nrm(ks[21], (DEPTH, D_MODEL), 0.02),
        "mlp_w1": nrm(ks[22], (DEPTH, D_MODEL, D_FF), D_MODEL ** -0.5),
        "mlp_w2": nrm(ks[23], (DEPTH, D_FF, D_MODEL), D_FF ** -0.5),
    }


def reference(x_prompt, x_sample, state_mlstm_C, state_mlstm_n, state_mlstm_m, cache_win_k, cache_win_v,
              ml_norm_g, ml_w_in, ml_b_i, ml_b_f, ml_head_g, ml_w_out,
              kv_norm_g, w_kv, k_norm_g,
              att_norm_g, att_w_q, q_norm_g, att_sinks, att_w_o,
              mlp_norm_g, mlp_w1, mlp_w2):
    B, L, _ = x_prompt.shape
    C0 = jnp.zeros((N_A_LAYERS, B, ML_HEADS, ML_DK, ML_DV), jnp.float32)
    n0 = jnp.zeros((N_A_LAYERS, B, ML_HEADS, ML_DK), jnp.float32)
    m0 = jnp.zeros((N_A_LAYERS, B, ML_HEADS), jnp.float32)
    prompt_chunk = ML_CHUNK if L % ML_CHUNK == 0 else L
    y_prompt, p_C, p_n, p_m, p_wk, p_wv = run_trunk(
        x_prompt, C0, n0, m0, None, None, prompt_chunk,
        ml_norm_g, ml_w_in, ml_b_i, ml_b_f, ml_head_g, ml_w_out,
        kv_norm_g, w_kv, k_norm_g, att_norm_g, att_w_q, q_norm_g, att_sinks, att_w_o,
        mlp_norm_g, mlp_w1, mlp_w2)
    y_sample, s_C, s_n, s_m, s_wk, s_wv = run_trunk(
        x_sample, state_mlstm_C, state_mlstm_n, state_mlstm_m, cache_win_k, cache_win_v, x_sample.shape[1],
        ml_norm_g, ml_w_in, ml_b_i, ml_b_f, ml_head_g, ml_w_out,
        kv_norm_g, w_kv, k_norm_g, att_norm_g, att_w_q, q_norm_g, att_sinks, att_w_o,
        mlp_norm_g, mlp_w1, mlp_w2)
    return (y_prompt, y_sample, p_C, p_n, p_m, p_wk, p_wv, s_C, s_n, s_m, s_wk, s_wv)
```

```python
import numpy as np
import ml_dtypes
from contextlib import ExitStack
import concourse.bass as bass
import concourse.mybir as mybir
from concourse.bass_utils import run_bass_kernel_spmd

F32 = mybir.dt.float32
BF16 = mybir.dt.bfloat16
AF = mybir.ActivationFunctionType
ALU = mybir.AluOpType
AX = mybir.AxisListType

D = 1024
DFF = 4096
NH = 4
DK = 128
DV = 256
DVX = 257
EPS = 1e-6
NS = 16
ST = 64
HALF = 2048
SLOT = 8448
NSLOT = 3


def I(meth, *a, **k):
    return lambda e: getattr(e, meth)(*a, **k)


class Res:
    __slots__ = ("w", "r", "name")

    def __init__(self, name=""):
        self.w = None
        self.r = []
        self.name = name


class Sched:
    ENG = ("pe", "act", "dve", "pool", "sp")

    def __init__(self, nc, es):
        self.nc = nc
        self.es = es
        self.q = {e: [] for e in self.ENG}
        self.sems = {}
        self.cnt = {}
        self.seen = {e: {} for e in self.ENG}
        self.tag = ''
        self.tile = 'init'
        self.tagmap = {}
        for e in self.ENG:
            self.newsem(e)

    def newsem(self, key):
        self.sems[key] = self.es.enter_context(self.nc.semaphore("s_" + key))
        self.cnt[key] = 0
        return key

    def _waits(self, eng, reads, writes):
        waits = {}

        def need(dep):
            if dep is None:
                return
            k, v = dep
            if self.seen[eng].get(k, 0) >= v:
                return
            if waits.get(k, 0) < v:
                waits[k] = v
        for r in reads:
            need(r.w)
        for w in writes:
            need(w.w)
            for d in w.r:
                need(d)
        for k, v in waits.items():
            self.seen[eng][k] = v
        return list(waits.items())

    def op(self, eng, fn, reads=(), writes=(), dsem=None):
        waits = self._waits(eng, reads, writes)
        if dsem is not None:
            self.cnt[dsem] += 16
            me = (dsem, self.cnt[dsem])
            self.q[eng].append((waits, fn, (dsem, 16), self.tag))
        else:
            self.cnt[eng] += 1
            me = (eng, self.cnt[eng])
            self.q[eng].append((waits, fn, (eng, 1), self.tag))
        for r in reads:
            r.r.append(me)
        for w in writes:
            w.w = me
            w.r = []
        return me

    def dma_group(self, eng, fns, reads=(), writes=(), dsem=None):
        waits = self._waits(eng, reads, writes)
        for i, fn in enumerate(fns):
            self.cnt[dsem] += 16
            self.q[eng].append((waits if i == 0 else [], fn, (dsem, 16), self.tag))
        me = (dsem, self.cnt[dsem])
        for r in reads:
            r.r.append(me)
        for w in writes:
            w.w = me
            w.r = []
        return me

    def group(self, eng, fns, reads=(), writes=()):
        waits = self._waits(eng, reads, writes)
        self.cnt[eng] += 1
        me = (eng, self.cnt[eng])
        n = len(fns)
        for i, fn in enumerate(fns):
            self.q[eng].append((waits if i == 0 else [], fn, (eng, 1) if i == n - 1 else None, self.tag))
        for r in reads:
            r.r.append(me)
        for w in writes:
            w.w = me
            w.r = []
        return me

    def emit(self, block, final_waits):
        nc = self.nc
        sems = self.sems

        def run(eng_key):
            def body(e):
                for waits, fn, inc, tag in self.q[eng_key]:
                    for k, v in waits:
                        e.wait_ge(sems[k], v)
                    ins = fn(e)
                    try:
                        self.tagmap[ins.ins.name] = tag
                    except Exception:
                        pass
                    if inc is not None:
                        ins.then_inc(sems[inc[0]], inc[1])
                if eng_key == "sp":
                    for k, v in final_waits:
                        e.wait_ge(sems[k], v)
            return body
        block.tensor(run("pe"))
        block.scalar(run("act"))
        block.vector(run("dve"))
        block.gpsimd(run("pool"))
        block.sync(run("sp"))


def build_program():
    nc = bass.Bass("TRN2", target_bir_lowering=False)
    es = ExitStack()
    S = Sched(nc, es)
    _NC_CACHE['S'] = S

    def din(name, shape, dt=F32):
        return nc.dram_tensor(name, list(shape), dt, kind="ExternalInput").ap()

    def dout(name, shape, dt=F32):
        return nc.dram_tensor(name, list(shape), dt, kind="ExternalOutput").ap()

    xp = din("xp", [HALF, D])
    xpre = din("xpre", [HALF, D])
    flag_d = din("flag", [128, 1])
    xs_d = din("xs", [ST, D])
    sC_d = din("sC", [NS, NH, DK, DV])
    sn_d = din("sn", [NS, NH, DK])
    smT_d = din("smT", [NH, NS])
    cwk_d = din("cwk", [NS, 128, 256])
    cwv_d = din("cwv", [NS, 128, 256])
    w_in_d = din("w_in", [D, 3080])
    w_out_d = din("w_out", [D, D])
    w1_d = din("w1", [2, D, DFF])
    w2_d = din("w2", [2, DFF, D])
    w_kv_d = din("w_kv", [D, 512])
    w_q_d = din("w_q", [D, D])
    w_o_d = din("w_o", [D, D])
    gcols_d = din("gcols", [128, 48])
    bif_d = din("bif", [NH, 2])
    gk_d = din("gk_b", [128, 64])
    gq_d = din("gq_b", [128, 64])
    sink_d = din("sink_b", [128, 16])
    identb_d = din("ident_b", [128, 128], BF16)
    identf_d = din("ident_f", [128, 128])
    mlmask_d = din("mlmask", [128, 64], BF16)
    smask_d = din("smask", [64, 64], BF16)
    amask_d = din("amask", [128, 256], BF16)
    cmask_d = din("cmask", [128, 4], BF16)
    rowmask_d = din("rowmask", [64, 16])
    eye4_d = din("eye4", [4, 4])
    ones4_d = din("ones4", [4, 128])

    def dscr(name, shape):
        return nc.dram_tensor(name, list(shape), BF16, kind="Internal").ap()
    sc_in = dscr("sc_in", [D, 3080]); sc_out = dscr("sc_out", [D, D])
    sc_w1 = dscr("sc_w1", [2, D, DFF]); sc_w2 = dscr("sc_w2", [2, DFF, D])
    sc_kv = dscr("sc_kv", [D, 512]); sc_q = dscr("sc_q", [D, D]); sc_o = dscr("sc_o", [D, D])

    y_d = dout("y", [HALF, D])
    ys_d = dout("ys", [ST, D])
    pC_d = dout("pC", [NH, DK, DV])
    pn_d = dout("pn", [NH, DK])
    pm_d = dout("pm", [NH, 1])
    pwk_d = dout("pwk", [128, 256])
    pwv_d = dout("pwv", [128, 256])
    oC_d = dout("oC", [NS, NH, DK, DV])
    on_d = dout("on", [NS, NH, DK])
    omT_d = dout("omT", [NH, NS])
    owk_d = dout("owk", [NS, 128, 256])
    owv_d = dout("owv", [NS, 128, 256])

    def sb(name, shape, dt=F32):
        return es.enter_context(nc.sbuf_tensor("sb_" + name, list(shape), dt))

    xbufs = [(sb("xres%d" % i, [128, 4, D]), [Res("x%d_%d" % (i, j)) for j in range(4)]) for i in range(2)]
    xres, R_x = xbufs[0]
    TM = [(sb("tm%d" % i, [128, 4, D], BF16), [Res() for _ in range(4)]) for i in range(2)]
    FM = [(sb("fm%d" % i, [128, 8, 512], BF16), [Res() for _ in range(4)]) for i in range(2)]
    hT = sb("hT", [128, 16, 512], BF16); R_hT = [Res() for _ in range(16)]
    wslot = [(sb("ws%d" % i, [128, SLOT], BF16), Res(), S.newsem("w%d" % i)) for i in range(NSLOT)]
    ktmR = [(sb("k_tm%d" % i, [128, 4, 512], BF16), [Res() for _ in range(4)]) for i in range(2)]
    k_tm, R_ktm = ktmR[0]
    Vx = sb("Vx", [128, 4, NH, DVX], BF16); R_Vx = [Res() for _ in range(4)]
    Cst = sb("Cst", [128, NH, DVX]); R_Cst = Res()
    Cb = sb("Cb", [128, NH, DVX], BF16); R_Cb = Res()
    SwTb = [(sb("SwT%d" % i, [128, 1280], BF16), Res()) for i in range(2)]
    SwT = SwTb[0][0][:, 0:256].rearrange("p (h t) -> p h t", t=64); R_SwT = SwTb[0][1]
    grow = {n: (sb("g_" + n, [4, 512]), Res()) for n in ("ti", "A", "t1", "t2")}
    grow["tf"] = grow["t2"]; grow["gp"] = grow["ti"]
    gsm = {n: (sb("gs_" + n, [4, 16]), Res()) for n in ("cm", "Mq", "Mp", "dec")}
    ddg = sb("ddg", [4, NH, 16]); R_ddg = Res()
    carryA = sb("carryA", [4, 1]); R_cA = Res()
    carryM = sb("carryM", [4, 1]); R_cM = Res()
    ewlbR = [(sb("ewlb%d" % i, [128, 4, 8]), Res()) for i in range(2)]
    decallR = [(sb("decall%d" % i, [128, NH, 16]), Res()) for i in range(2)]
    ewlb, R_ewlb = ewlbR[0]
    decall, R_decall = decallR[0]
    small = sb("small", [128, 64]); R_small = Res()
    R_smk = [Res(), Res()]; R_smq = [Res(), Res()]; R_smh = [Res() for _ in range(4)]; R_sma = Res()
    ssq = sb("ssq", [128, 8]); R_ssq = [Res() for _ in range(4)]
    junk = sb("junk", [128, 256], BF16); R_junk = Res()
    kfR = [(sb("kf%d" % i, [128, 512]), Res()) for i in range(2)]
    knR = [(sb("kn%d" % i, [128, 256]), Res()) for i in range(2)]
    KtmR = [(sb("Ktm%d" % i, [128, 256], BF16), Res()) for i in range(2)]
    kT = sb("kT", [128, 2, 640], BF16); R_kT = [Res() for _ in range(5)]
    Vex = sb("Vex", [128, 5, 4, 65], BF16); R_Vex = [Res() for _ in range(5)]
    qfR = [(sb("qf%d" % i, [128, 512]), Res()) for i in range(2)]
    PT = [(sb("PT%d" % i, [128, 2, 512], BF16), Res()) for i in range(2)]
    gcols = sb("gcols", [128, 48]); bif = sb("bif", [4, 2]); bif15 = sb("bif15", [4, 2])
    gk_b = sb("gk_b", [128, 64]); gq_b = sb("gq_b", [128, 64]); sink_b = sb("sink_b", [128, 16])
    ident_b = sb("ident_b", [128, 128], BF16); ident_f = sb("ident_f", [128, 128])
    mlmask = sb("mlmask", [128, 64], BF16); smask = sb("smask", [64, 64], BF16)
    amask = sb("amask", [128, 256], BF16); cmask = sb("cmask", [128, 4], BF16)
    rowmask = sb("rowmask", [64, 16]); eye4 = sb("eye4", [4, 4]); ones4 = sb("ones4", [4, 128])
    flag = sb("flag", [128, 1])
    R_const = Res("const")
    s_m0 = sb("s_m0", [4, NS]); R_sm0 = Res()
    s_mo = sb("s_mo", [4, NS]); R_smo = Res()
    Cin = [(None, None, S.newsem("ci%d" % i)) for i in range(2)]
    fdummy = sb("fdummy", [128, 2]); R_fd = Res()
    nout = sb("nout", [128, NS * NH]); R_nout = Res()
    noutT = sb("noutT", [NS * NH, 128]); R_noutT = Res()
    samp_sems = []
    R_cvb = Res()
    KcT = sb("KcT", [128, NS, 128], BF16); R_KcT = Res()
    PTc = sb("PTc", [128, NS, 4, ST], BF16); R_PTc = [Res() for _ in range(NS)]
    PTn = sb("PTn", [64, 4, ST], BF16); R_PTn = Res()

    ps_all = es.enter_context(nc.psum_tensor("ps_all", [128, 8, 512], F32))
    R_ps = [Res("ps%d" % i) for i in range(8)]
    ps_ctr = [0]

    ps_reserved = set()

    def psum(reserve=False):
        while True:
            i = ps_ctr[0] % 8
            ps_ctr[0] += 1
            if i not in ps_reserved:
                break
        if reserve:
            ps_reserved.add(i)
        return ps_all[:, i, :], R_ps[i]

    def psum_release(pr):
        ps_reserved.discard(R_ps.index(pr))

    tm_ctr = [0]
    fm_ctr = [0]

    def next_tm():
        i = tm_ctr[0] % len(TM); tm_ctr[0] += 1
        return TM[i]

    def next_fm():
        i = fm_ctr[0] % len(FM); fm_ctr[0] += 1
        return FM[i]

    s_ld = S.newsem("ld")
    s_st = S.newsem("st")
    s_ldx = [S.newsem("ldx%d" % i) for i in range(4)]
    s_sty = [S.newsem("sty%d" % i) for i in range(4)]
    s_ck = S.newsem("ck")
    s_cv = S.newsem("cv")
    s_co = [S.newsem("co%d" % i) for i in range(2)]
    s_c = S.newsem("cst")

    cfns = []
    for dst, src in ((gcols, gcols_d), (bif, bif_d), (gk_b, gk_d), (gq_b, gq_d), (sink_b, sink_d),
                     (ident_b, identb_d), (ident_f, identf_d), (mlmask, mlmask_d), (smask, smask_d),
                     (amask, amask_d), (cmask, cmask_d), (rowmask, rowmask_d), (eye4, eye4_d),
                     (ones4, ones4_d), (flag, flag_d), (s_m0, smT_d)):
        cfns.append(I("dma_start", out=dst[:], in_=src[:]))
    S.dma_group("sp", cfns, writes=[R_const], dsem=s_c)
    S.op("dve", I("tensor_scalar", out=bif15[:], in0=bif[:], scalar1=1.0 / 15.0, scalar2=None, op0=ALU.mult),
         reads=[R_const], writes=[R_const])
    S.op("act", I("activation", out=sink_b[:], in_=sink_b[:], func=AF.Exp), reads=[R_const], writes=[R_const])
    S.op("dve", I("memset", Cst[:], 0.0), writes=[R_Cst])
    S.op("dve", I("memset", carryA[:], 0.0), writes=[R_cA])
    S.op("dve", I("memset", ddg[:], 0.0), writes=[R_ddg])
    S.op("dve", I("memset", carryM[:], 0.0), writes=[R_cM])
    S.op("dve", I("memset", Vx[:], 0.0), writes=R_Vx)
    S.op("dve", I("memset", Vex[:], 1.0), writes=R_Vex)
    S.op("dve", I("memset", kT[:], 0.0), writes=R_kT)
    S.op("dve", I("memset", PTc[:], 0.0), writes=R_PTc)

    wq = []
    conv = {}
    conv_sem = {}
    wstate = {"issued": 0, "used": 0}

    def wview(slot_t, kc, cols):
        return slot_t[:, 0:kc * cols].rearrange("p (k c) -> p k c", c=cols)

    def chunk_dmas(kind, layer, mode):
        sc = (mode == "sc")

        def rows(w, k, c0, c1):
            return w[k * 128:(k + 1) * 128, c0:c1]
        Win = sc_in if sc else w_in_d
        out = []
        if kind == "inB":
            for k in range(8):
                if mode == "part":
                    out.append((k, 1024, 1032, rows(Win, k, 3072, 3080)))
                else:
                    out.append((k, 0, 1032, rows(Win, k, 2048, 3080)))
            return 8, 1032, out, ["in"]
        if kind == "inQK":
            for k in range(8):
                if mode == "part":
                    out.append((k, 512, 1024, rows(Win, k, 512, 1024)))
                else:
                    out.append((k, 0, 1024, rows(Win, k, 0, 1024)))
            return 8, 1024, out, ["in"]
        if kind == "inV":
            for k in range(8):
                out.append((k, 0, 1024, rows(Win, k, 1024, 2048)))
            return 8, 1024, out, ["in"]
        if kind == "out":
            for k in range(8):
                out.append((k, 0, 1024, rows(sc_out if sc else w_out_d, k, 0, 1024)))
            return 8, 1024, out, ["out"]
        if kind.startswith("w1_"):
            c0 = int(kind[3:]) * 1024
            W = sc_w1 if sc else w1_d
            for k in range(8):
                out.append((k, 0, 1024, W[layer, k * 128:(k + 1) * 128, c0:c0 + 1024]))
            return 8, 1024, out, ["w1_%d" % layer]
        if kind.startswith("w2_"):
            r0 = int(kind[3:]) * 1024
            W = sc_w2 if sc else w2_d
            for k4 in range(2):
                src = W[layer, r0 + k4 * 512:r0 + (k4 + 1) * 512, :].rearrange("(k p) c -> p k c", p=128)
                out.append(((k4 * 4, k4 * 4 + 4), 0, 1024, src))
            return 8, 1024, out, ["w2_%d" % layer]
        if kind == "kv":
            for k in range(8):
                out.append((k, 0, 512, rows(sc_kv if sc else w_kv_d, k, 0, 512)))
            return 8, 512, out, ["kv"]
        if kind == "q":
            for k in range(8):
                out.append((k, 0, 1024, rows(sc_q if sc else w_q_d, k, 0, 1024)))
            return 8, 1024, out, ["q"]
        if kind == "o":
            for k in range(8):
                out.append((k, 0, 1024, rows(sc_o if sc else w_o_d, k, 0, 1024)))
            return 8, 1024, out, ["o"]
        raise ValueError(kind)

    def plan_chunk(kind, layer=0, mode="sc"):
        wq.append((kind, layer, mode))

    def issue_load(idx):
        kind, layer, mode = wq[idx]
        slot_t, slot_r, slot_s = wslot[idx % NSLOT]
        kc, cols, dmas, deps = chunk_dmas(kind, layer, mode)
        v = wview(slot_t, kc, cols)

        def dst_of(k, c0, c1):
            return v[:, k[0]:k[1], c0:c1] if isinstance(k, tuple) else v[:, k, c0:c1]
        fns = [I("dma_start", out=dst_of(k, c0, c1), in_=src, max_dma_last_dim=8192) for (k, c0, c1, src) in dmas]
        if mode == "sc":
            S.dma_group("sp", fns, reads=[conv[d] for d in deps], writes=[slot_r], dsem=slot_s)
        else:
            S.dma_group("pool", fns, writes=[slot_r], dsem=slot_s)
        if mode == "f32wb":
            _a, _b, sdmas, _c = chunk_dmas(kind, layer, "sc")
            wfns = [I("dma_start", out=ssrc, in_=dst_of(k, c0, c1)) for (k, c0, c1, ssrc) in sdmas]
            name = deps[0]
            if name not in conv:
                conv[name] = Res("sc_" + name)
                conv_sem[name] = S.newsem("wb_" + name)
            S.dma_group("sp", wfns, reads=[slot_r], writes=[conv[name]], dsem=conv_sem[name])

    def wget(kind):
        idx = wstate["used"]
        assert wq[idx][0] == kind, (wq[idx], kind)
        while wstate["issued"] < min(len(wq), idx + NSLOT):
            issue_load(wstate["issued"]); wstate["issued"] += 1
        slot_t, slot_r, _ = wslot[idx % NSLOT]
        kc, cols, _d, _e = chunk_dmas(*wq[idx])
        return wview(slot_t, kc, cols), slot_r

    def wdone():
        wstate["used"] += 1

    def pipeline(N, phases, lag=1):
        for step in range(N + lag * (len(phases) - 1)):
            for p, ph in enumerate(phases):
                u = step - p * lag
                if 0 <= u < N:
                    ph(u)

    def evac(i, out, in_, reads, writes, scale=None):
        if i % 2 == 0:
            if scale is None:
                S.op("act", I("activation", out=out, in_=in_, func=AF.Copy), reads=reads, writes=writes)
            else:
                S.op("act", I("activation", out=out, in_=in_, func=AF.Copy, scale=scale), reads=reads, writes=writes)
        else:
            if scale is None:
                S.op("dve", I("tensor_copy", out=out, in_=in_), reads=reads, writes=writes)
            else:
                S.op("dve", I("tensor_scalar", out=out, in0=in_, scalar1=scale, scalar2=None, op0=ALU.mult),
                     reads=reads, writes=writes)

    def rsqrt_small(dst, src, n, scale, reads_writes):
        S.op("act", I("activation", out=dst, in_=src, func=AF.Ln, scale=scale, bias=eps_c[0:dst.shape[0], :]),
             reads=reads_writes + [R_const], writes=reads_writes)
        S.op("act", I("activation", out=dst, in_=dst, func=AF.Exp, scale=-0.5),
             reads=reads_writes, writes=reads_writes)

    eps_c = sb("eps_c", [128, 1])
    S.op("dve", I("memset", eps_c[:], EPS), writes=[R_const])

    def norm_T(NB, PTK, gidx):
        tm, tmr = next_tm()
        fm, fmr = next_fm()
        P = slice(0, PTK)
        for b in range(NB):
            S.op("act", I("activation", out=tm[P, b, :], in_=xres[P, b, :], func=AF.Square, accum_out=ssq[P, b:b + 1]),
                 reads=[R_x[b]], writes=[tmr[b], R_ssq[b]])
            rsqrt_small(ssq[P, b:b + 1], ssq[P, b:b + 1], 1, 1.0 / D, [R_ssq[b]])
            S.op("dve", I("tensor_scalar", out=tm[P, b, :], in0=xres[P, b, :], scalar1=ssq[P, b:b + 1], scalar2=None, op0=ALU.mult),
                 reads=[R_x[b], R_ssq[b]], writes=[tmr[b]])
            tm_to_fm_blk(tm, tmr, fm, fmr, b, PTK, gidx)
        return fm, fmr

    def norm_T2(NB, PTK, g1, g2):
        tm, tmr = next_tm()
        fm1, fmr1 = next_fm()
        fm2, fmr2 = next_fm()
        P = slice(0, PTK)
        for b in range(NB):
            S.op("act", I("activation", out=tm[P, b, :], in_=xres[P, b, :], func=AF.Square, accum_out=ssq[P, b:b + 1]),
                 reads=[R_x[b]], writes=[tmr[b], R_ssq[b]])
            rsqrt_small(ssq[P, b:b + 1], ssq[P, b:b + 1], 1, 1.0 / D, [R_ssq[b]])
            S.op("dve", I("tensor_scalar", out=tm[P, b, :], in0=xres[P, b, :], scalar1=ssq[P, b:b + 1], scalar2=None, op0=ALU.mult),
                 reads=[R_x[b], R_ssq[b]], writes=[tmr[b]])
            pt, pr = psum()
            ptb = pt.bitcast(BF16)
            fns = [I("transpose", out=ptb[:, k * PTK:(k + 1) * PTK], in_=tm[0:PTK, b, k * 128:(k + 1) * 128], identity=ident_b[0:PTK, 0:PTK])
                   for k in range(8)]
            S.group("pe", fns, reads=[tmr[b], R_const], writes=[pr])
            src = ptb[:, 0:8 * PTK].rearrange("p (k t) -> p k t", t=PTK)
            for (fm, fmr, g) in ((fm1, fmr1, g1), (fm2, fmr2, g2)):
                S.op("dve", I("tensor_tensor", out=fm[:, :, b * PTK:(b + 1) * PTK], in0=src,
                              in1=gcols[:, g * 8:(g + 1) * 8].unsqueeze(2).to_broadcast([128, 8, PTK]), op=ALU.mult),
                     reads=[pr, R_const], writes=[fmr[b]])
        return fm1, fmr1, fm2, fmr2

    def tm_to_fm_blk(tm, tmr, fm, fmr, b, PTK, gidx, use_act=False):
        pt, pr = psum()
        ptb = pt.bitcast(BF16)
        fns = [I("transpose", out=ptb[:, k * PTK:(k + 1) * PTK], in_=tm[0:PTK, b, k * 128:(k + 1) * 128], identity=ident_b[0:PTK, 0:PTK])
               for k in range(8)]
        S.group("pe", fns, reads=[tmr[b], R_const], writes=[pr])
        dst = fm[:, :, b * PTK:(b + 1) * PTK]
        src = ptb[:, 0:8 * PTK].rearrange("p (k t) -> p k t", t=PTK)
        if gidx is not None:
            S.op("dve", I("tensor_tensor", out=dst, in0=src, in1=gcols[:, gidx * 8:(gidx + 1) * 8].unsqueeze(2).to_broadcast([128, 8, PTK]),
                          op=ALU.mult), reads=[pr, R_const], writes=[fmr[b]])
        elif use_act:
            S.op("act", I("activation", out=dst, in_=src, func=AF.Copy), reads=[pr], writes=[fmr[b]])
        else:
            S.op("dve", I("tensor_copy", out=dst, in_=src), reads=[pr], writes=[fmr[b]])

    def tm_to_fm(tm, tmr, fm, fmr, NB, PTK, gidx):
        for b in range(NB):
            tm_to_fm_blk(tm, tmr, fm, fmr, b, PTK, gidx, use_act=(b % 2 == 0))

    def proj_tm(fm, fmr, NB, PTK, wv, wr, c0, ncols, consume):
        for b in range(NB):
            pt, pr = psum()
            fns = [I("matmul", pt[0:PTK, 0:ncols], lhsT=fm[:, k, b * PTK:(b + 1) * PTK],
                                                 rhs=wv[:, k, c0:c0 + ncols], start=(k == 0), stop=(k == 7))
                   for k in range(8)]
            S.group("pe", fns, reads=[fmr[b], wr], writes=[pr])
            consume(b, pt, pr)

    def resid_add(NB, PTK, fm, fmr, wv, wr, nk):
        for b in range(NB):
            for half in range(2):
                pt, pr = psum()
                fns = [I("matmul", pt[0:PTK, :], lhsT=fm[:, k, b * PTK:(b + 1) * PTK], rhs=wv[:, k, half * 512:(half + 1) * 512],
                         start=(k == 0), stop=(k == nk - 1)) for k in range(nk)]
                S.group("pe", fns, reads=[fmr[b], wr], writes=[pr])
                S.op("dve", I("tensor_tensor", out=xres[0:PTK, b, half * 512:(half + 1) * 512], in0=pt[0:PTK, :],
                              in1=xres[0:PTK, b, half * 512:(half + 1) * 512], op=ALU.add),
                     reads=[pr, R_x[b]], writes=[R_x[b]])

    MLP_ORDER = ("w1_0", "w1_1", "w2_0", "w1_2", "w2_1", "w1_3", "w2_2", "w2_3")

    def mlp(NB, PTK, layer, hook=None):
        T = NB * PTK
        S.tag = S.tile + ':mlp%d' % layer
        fm, fmr = norm_T(NB, PTK, 1 if layer == 0 else 4)

        def w1_part(part):
            hb = (part % 2) * 8
            wv, wr = wget("w1_%d" % part)
            for f in range(8):
                pt, pr = psum()
                fns = [I("matmul", pt[:, 0:T], lhsT=wv[:, k, f * 128:(f + 1) * 128], rhs=fm[:, k, 0:T],
                         start=(k == 0), stop=(k == 7)) for k in range(8)]
                S.group("pe", fns, reads=fmr[0:NB] + [wr], writes=[pr])
                S.op("act", I("activation", out=hT[:, hb + f, 0:T], in_=pt[:, 0:T], func=AF.Relu), reads=[pr], writes=[R_hT[hb + f]])
                S.op("dve", I("tensor_tensor", out=hT[:, hb + f, 0:T], in0=hT[:, hb + f, 0:T], in1=hT[:, hb + f, 0:T], op=ALU.mult),
                     reads=[R_hT[hb + f]], writes=[R_hT[hb + f]])
            wdone()

        def w2_part(part):
            hb = (part % 2) * 8
            wv, wr = wget("w2_%d" % part)
            for b in range(NB):
                for half in range(2):
                    pt, pr = psum()
                    fns = [I("matmul", pt[0:PTK, :], lhsT=hT[:, hb + k, b * PTK:(b + 1) * PTK], rhs=wv[:, k, half * 512:(half + 1) * 512],
                             start=(k == 0), stop=(k == 7)) for k in range(8)]
                    S.group("pe", fns, reads=R_hT[hb:hb + 8] + [wr], writes=[pr])
                    S.op("dve", I("tensor_tensor", out=xres[0:PTK, b, half * 512:(half + 1) * 512], in0=pt[0:PTK, :],
                                  in1=xres[0:PTK, b, half * 512:(half + 1) * 512], op=ALU.add),
                         reads=[pr, R_x[b]], writes=[R_x[b]])
            wdone()
        for ii, nm in enumerate(MLP_ORDER):
            if hook is not None and ii == len(MLP_ORDER) - 2:
                hook()
            (w1_part if nm.startswith("w1") else w2_part)(int(nm[3:]))

    def gates(fm, fmr, wv, wr, T, sample):
        g = {n: grow[n][0] for n in grow}
        gr = {n: grow[n][1] for n in grow}
        for nm, col, bcol in (("ti", 1024, 0), ("tf", 1028, 1)):
            pt, pr = psum()
            fns = [I("matmul", pt[0:4, 0:T], lhsT=wv[:, k, col:col + 4], rhs=fm[:, k, 0:T],
                                                      start=(k == 0), stop=(k == 7)) for k in range(8)]
            S.group("pe", fns, reads=fmr + [wr], writes=[pr])
            yield
            S.op("act", I("activation", out=g[nm][:, 0:T], in_=pt[0:4, 0:T], func=AF.Tanh,
                                                                      scale=1.0 / 15.0, bias=bif15[:, bcol:bcol + 1]),
                 reads=[pr, R_const], writes=[gr[nm]])
            yield
        S.op("act", I("activation", out=g["t1"][:, 0:T], in_=g["tf"][:, 0:T], func=AF.Exp, scale=-15.0),
             reads=[gr["tf"]], writes=[gr["t1"]])
        yield
        S.op("act", I("activation", out=g["t1"][:, 0:T], in_=g["t1"][:, 0:T], func=AF.Ln, bias=1.0),
             reads=[gr["t1"]], writes=[gr["t1"]])
        yield
        cm, Mq, Mp, dec = (gsm[n][0] for n in ("cm", "Mq", "Mp", "dec"))
        r_cm, r_Mq, r_Mp, r_dec = (gsm[n][1] for n in ("cm", "Mq", "Mp", "dec"))
        if not sample:
            NC = 1
            CL = T
            S.op("dve", I("tensor_tensor_scan", out=g["A"][:, 0:T], data0=g["t1"][:, 0:T], data1=g["t1"][:, 0:T],
                                                       initial=carryA[:, 0:1], op0=ALU.add, op1=ALU.max),
                 reads=[gr["t1"], R_cA], writes=[gr["A"]])
            yield
        else:
            NC = NS
            CL = 4
            A3 = g["A"][:, 0:T].rearrange("p (c j) -> p c j", j=4)
            l3 = g["t1"][:, 0:T].rearrange("p (c j) -> p c j", j=4)
            S.op("dve", I("tensor_copy", out=A3[:, :, 0:1], in_=l3[:, :, 0:1]), reads=[gr["t1"]], writes=[gr["A"]])
            yield
            for j in range(1, 4):
                S.op("dve", I("tensor_tensor", out=A3[:, :, j:j + 1], in0=A3[:, :, j - 1:j], in1=l3[:, :, j:j + 1],
                                                                 op=ALU.add),
                     reads=[gr["t1"], gr["A"]], writes=[gr["A"]])
                yield
        S.op("dve", I("scalar_tensor_tensor", out=g["gp"][:, 0:T], in0=g["ti"][:, 0:T], scalar=15.0, in1=g["A"][:, 0:T],
                                                     op0=ALU.mult, op1=ALU.add),
             reads=[gr["ti"], gr["A"]], writes=[gr["gp"]])
        yield
        gp3 = g["gp"][:, 0:T].rearrange("p (c j) -> p c j", j=CL)
        A3 = g["A"][:, 0:T].rearrange("p (c j) -> p c j", j=CL)
        S.op("dve", I("tensor_reduce", out=cm[:, 0:NC], in_=gp3, axis=AX.X, op=ALU.max),
             reads=[gr["gp"]], writes=[r_cm])
        yield
        if not sample:
            S.op("dve", I("tensor_tensor_scan", out=Mq[:, 0:NC], data0=cm[:, 0:NC], data1=cm[:, 0:NC],
                                                       initial=carryM[:, 0:1], op0=ALU.max, op1=ALU.max),
                 reads=[r_cm, R_cM], writes=[r_Mq])
            yield
            S.op("dve", I("tensor_copy", out=Mp[:, 0:1], in_=carryM[:, 0:1]), reads=[R_cM], writes=[r_Mp])
            yield
            if NC > 1:
                S.op("dve", I("tensor_copy", out=Mp[:, 1:NC], in_=Mq[:, 0:NC - 1]), reads=[r_Mq, r_Mp], writes=[r_Mp])
                yield
        else:
            S.op("dve", I("tensor_tensor", out=Mq[:, 0:NC], in0=cm[:, 0:NC], in1=s_m0[:, 0:NC], op=ALU.max),
                 reads=[r_cm, R_const], writes=[r_Mq])
            yield
            S.op("dve", I("tensor_copy", out=Mp[:, 0:NC], in_=s_m0[:, 0:NC]), reads=[R_const], writes=[r_Mp])
            yield
        S.op("dve", I("tensor_tensor", out=dec[:, 0:NC], in0=Mp[:, 0:NC], in1=Mq[:, 0:NC], op=ALU.subtract),
             reads=[r_Mp, r_Mq], writes=[r_dec])
        yield
        S.op("act", I("activation", out=dec[:, 0:NC], in_=dec[:, 0:NC], func=AF.Exp), reads=[r_dec], writes=[r_dec])
        yield
        Mqb = Mq[:, 0:NC].unsqueeze(2).to_broadcast([4, NC, CL])
        t1_3 = g["t1"][:, 0:T].rearrange("p (c j) -> p c j", j=CL)
        t2_3 = g["t2"][:, 0:T].rearrange("p (c j) -> p c j", j=CL)
        S.op("dve", I("tensor_tensor", out=t1_3, in0=gp3, in1=Mqb, op=ALU.subtract),
             reads=[gr["gp"], r_Mq, gr["t1"]], writes=[gr["t1"]])
        yield
        S.op("act", I("activation", out=g["t1"][:, 0:T], in_=g["t1"][:, 0:T], func=AF.Exp), reads=[gr["t1"]], writes=[gr["t1"]])
        yield
        S.op("dve", I("tensor_tensor", out=t2_3, in0=A3, in1=Mqb, op=ALU.subtract),
             reads=[gr["A"], r_Mq], writes=[gr["t2"]])
        yield
        S.op("act", I("activation", out=g["t2"][:, 0:T], in_=g["t2"][:, 0:T], func=AF.Exp), reads=[gr["t2"]], writes=[gr["t2"]])
        yield
        PTK = min(T, 128)
        NB = T // PTK
        pt, pr = psum()
        fns = []
        for b in range(NB):
            for j, nm in enumerate(("t1", "t2")):
                fns.append(I("transpose", out=pt[0:PTK, b * 8 + j * 4:b * 8 + j * 4 + 4],
                                                                   in_=g[nm][:, b * PTK:(b + 1) * PTK],
                                                                   identity=ident_f[0:4, 0:4]))
        S.group("pe", fns, reads=[gr["t1"], gr["t2"], R_const], writes=[pr])
        yield
        S.op("dve", I("tensor_copy", out=ewlb[0:PTK, 0:NB, :], in_=pt[0:PTK, 0:NB * 8].rearrange("p (b j) -> p b j", j=8)),
             reads=[pr], writes=[R_ewlb])
        yield
        S.op("dve", I("tensor_tensor", out=ddg[:, :, 0:NC], in0=dec[:, 0:NC].unsqueeze(1).to_broadcast([4, NH, NC]),
                                              in1=eye4[:, :].unsqueeze(2).to_broadcast([4, NH, NC]), op=ALU.mult),
             reads=[r_dec, R_const], writes=[R_ddg])
        yield
        pt2, pr2 = psum()
        S.group("pe", [I("matmul", pt2[:, 0:NH * 16], lhsT=ones4[:, :], rhs=ddg[:, :, :].rearrange("p h c -> p (h c)"),
                                          start=True, stop=True)],
                reads=[R_ddg, R_const], writes=[pr2])
        yield
        S.op("dve", I("tensor_copy", out=decall[:, :, :], in_=pt2[:, 0:NH * 16].rearrange("p (h c) -> p h c", c=16)),
             reads=[pr2], writes=[R_decall])
        yield
        if not sample:
            S.op("dve", I("tensor_copy", out=carryA[:, 0:1], in_=g["A"][:, T - 1:T]), reads=[gr["A"]], writes=[R_cA])
            yield
            S.op("dve", I("tensor_copy", out=carryM[:, 0:1], in_=Mq[:, NC - 1:NC]), reads=[r_Mq], writes=[R_cM])
            yield
        else:
            S.op("dve", I("tensor_tensor", out=s_mo[:, 0:NC], in0=Mq[:, 0:NC], in1=A3[:, :, 3:4].rearrange("p c j -> p (c j)"),
                                                  op=ALU.subtract),
                 reads=[r_Mq, gr["A"]], writes=[R_smo])
            yield

    def mlstm(NB, PTK, do_out, sample, pre=None):
        T = NB * PTK
        S.tag = S.tile + ':ml_proj'
        fm, fmr = pre if pre is not None else norm_T(NB, PTK, 0)
        wv, wr = wget("inB")
        gg = gates(fm, fmr, wv, wr, T, sample)

        def gstep(n=1):
            for _ in range(n):
                next(gg, None)
        gstep(4)
        if do_out:
            otm, otr = next_tm()
            for half in range(2):
                def cons(b, pt, pr, half=half):
                    S.op("act", I("activation", out=otm[0:PTK, b, half * 512:(half + 1) * 512], in_=pt[0:PTK, :],
                                                       func=AF.Sigmoid), reads=[pr], writes=[otr[b]])
                    gstep(2)
                proj_tm(fm, fmr, NB, PTK, wv, wr, half * 512, 512, cons)
        wdone()
        wv, wr = wget("inQK")
        if do_out:
            qk, qkr = next_fm()
            for c in range(8):
                pt, pr = psum()
                fns = [I("matmul", pt[:, 0:T], lhsT=wv[:, k, c * 128:(c + 1) * 128], rhs=fm[:, k, 0:T],
                                                     start=(k == 0), stop=(k == 7)) for k in range(8)]
                S.group("pe", fns, reads=fmr + [wr], writes=[pr])
                evac(c, qk[:, c, 0:T], pt[:, 0:T], [pr], qkr, scale=(DK ** -0.5 if c >= 4 else None))
                gstep(2)

        def cons_k(b, pt, pr):
            evac(b, k_tm[0:PTK, b, :], pt[0:PTK, :], [pr], [R_ktm[b]], scale=DK ** -0.5)
            gstep(3)
        proj_tm(fm, fmr, NB, PTK, wv, wr, 512, 512, cons_k)
        wdone()
        for _ in gg:
            pass
        wv, wr = wget("inV")
        for half in range(2):
            def cons_v(b, pt, pr, half=half):
                for j in range(2):
                    h = half * 2 + j
                    evac(j, Vx[0:PTK, b, h, 0:DV], pt[0:PTK, j * 256:(j + 1) * 256], [pr, R_ewlb], [R_Vx[b]],
                         scale=ewlb[0:PTK, b, h:h + 1])
            proj_tm(fm, fmr, NB, PTK, wv, wr, half * 512, 512, cons_v)
        for b in range(NB):
            S.op("dve", I("tensor_copy", out=Vx[0:PTK, b, :, DV:DVX], in_=ewlb[0:PTK, b, 0:4].unsqueeze(2)),
                 reads=[R_ewlb], writes=[R_Vx[b]])
        wdone()
        S.tag = S.tile + ':ml_core'
        if sample:
            mlstm_core_sample(qk, qkr, otm, otr)
        else:
            mlstm_core(NB, do_out, qk if do_out else None, qkr if do_out else None,
                       otm if do_out else None, otr if do_out else None)
        if do_out:
            S.tag = S.tile + ':ml_out'
            fo, fo_r = next_fm()
            tm_to_fm(otm, otr, fo, fo_r, NB, PTK, 5)
            wv, wr = wget("out")
            resid_add(NB, PTK, fo, fo_r, wv, wr, 8)
            wdone()

    def selx(i):
        nonlocal xres, R_x
        xres, R_x = xbufs[i]

    def sel(par):
        nonlocal ewlb, R_ewlb, decall, R_decall, k_tm, R_ktm
        ewlb, R_ewlb = ewlbR[par]
        decall, R_decall = decallR[par]
        k_tm, R_ktm = ktmR[par]

    pre_fm = {}

    def prefix_A(k, r0, NB):
        sel(k % 2)
        S.tile = 'pre%d' % r0
        S.tag = S.tile + ':ml_proj'
        load_x(xpre[r0:r0 + NB * 128, :], NB, 128)
        T = NB * 128
        fm, fmr = norm_T(NB, 128, 0)
        wv, wr = wget("inB")
        gg = gates(fm, fmr, wv, wr, T, False)
        for _ in range(4):
            next(gg, None)
        wdone()
        wv, wr = wget("inQK")

        def cons_k(b, pt, pr):
            evac(b, k_tm[:, b, :], pt[:, :], [pr], [R_ktm[b]], scale=DK ** -0.5)
            for _ in range(4):
                next(gg, None)
        proj_tm(fm, fmr, NB, 128, wv, wr, 512, 512, cons_k)
        wdone()
        for _ in gg:
            pass
        pre_fm[k] = (fm, fmr)

    def prefix_B(k, r0, NB):
        sel(k % 2)
        S.tile = 'pre%d' % r0
        S.tag = S.tile + ':ml_core'
        fm, fmr = pre_fm.pop(k)
        wv, wr = wget("inV")
        for half in range(2):
            def cons_v(b, pt, pr, half=half):
                for j in range(2):
                    h = half * 2 + j
                    evac(j, Vx[:, b, h, 0:DV], pt[:, j * 256:(j + 1) * 256], [pr, R_ewlb], [R_Vx[b]],
                         scale=ewlb[:, b, h:h + 1])
            proj_tm(fm, fmr, NB, 128, wv, wr, half * 512, 512, cons_v)
        for b in range(NB):
            S.op("dve", I("tensor_copy", out=Vx[:, b, :, DV:DVX], in_=ewlb[:, b, 0:4].unsqueeze(2)),
                 reads=[R_ewlb], writes=[R_Vx[b]])
        wdone()
        mlstm_core(NB, False, None, None, None, None)

    def head_out(pn, pnr, b, h, PTK, otm, otr):
        P = slice(0, PTK)
        sm = small
        c0 = h * 8
        S.op("dve", I("tensor_scalar", out=sm[P, c0:c0 + 1], in0=pn[P, DV:DVX], scalar1=-1.0, scalar2=None, op0=ALU.mult),
             reads=[pnr], writes=[R_smh[h]])
        S.op("dve", I("tensor_tensor", out=sm[P, c0:c0 + 1], in0=sm[P, c0:c0 + 1], in1=pn[P, DV:DVX], op=ALU.max),
             reads=[pnr, R_smh[h]], writes=[R_smh[h]])
        S.op("dve", I("tensor_tensor", out=sm[P, c0:c0 + 1], in0=sm[P, c0:c0 + 1], in1=ewlb[P, b, 4 + h:5 + h], op=ALU.max),
             reads=[R_ewlb, R_smh[h]], writes=[R_smh[h]])
        S.op("dve", I("reciprocal", out=sm[P, c0 + 1:c0 + 2], in_=sm[P, c0:c0 + 1]), reads=[R_smh[h]], writes=[R_smh[h]])
        S.op("act", I("activation", out=junk[P, 0:DV], in_=pn[P, 0:DV], func=AF.Square, scale=sm[P, c0 + 1:c0 + 2],
                                           accum_out=sm[P, c0 + 2:c0 + 3]),
             reads=[pnr, R_smh[h]], writes=[R_junk, R_smh[h]])
        rsqrt_small(sm[P, c0 + 3:c0 + 4], sm[P, c0 + 2:c0 + 3], 1, 1.0 / DV, [R_smh[h]])
        S.op("dve", I("tensor_tensor", out=sm[P, c0 + 4:c0 + 5], in0=sm[P, c0 + 3:c0 + 4], in1=sm[P, c0 + 1:c0 + 2], op=ALU.mult),
             reads=[R_smh[h]], writes=[R_smh[h]])
        S.op("dve", I("scalar_tensor_tensor", out=otm[P, b, h * DV:(h + 1) * DV], in0=pn[P, 0:DV], scalar=sm[P, c0 + 4:c0 + 5],
                                                     in1=otm[P, b, h * DV:(h + 1) * DV], op0=ALU.mult, op1=ALU.mult),
             reads=[pnr, R_smh[h], otr[b]], writes=[otr[b]])

    def mlstm_core(NB, do_out, qk, qkr, otm, otr):
        T = NB * 128
        S.op("dve", I("tensor_tensor", out=Cst[:, :, :], in0=Cst[:, :, :], in1=decall[:, :, 0:1].to_broadcast([128, NH, DVX]),
                      op=ALU.mult), reads=[R_decall, R_Cst], writes=[R_Cst])
        if do_out:
            S.op("act", I("activation", out=Cb[:, :, :], in_=Cst[:, :, :], func=AF.Copy), reads=[R_Cst], writes=[R_Cb])
            offs = [sum((NB - ii) * 128 for ii in range(i)) for i in range(NB)]
            def phA(h):
                sw, swr = SwTb[h % 2]
                for i in range(NB):
                    n = (NB - i) * 128
                    pt, pr = psum()
                    S.group("pe", [I("matmul", pt[:, 0:n], lhsT=qk[:, 4 + h, i * 128:(i + 1) * 128], rhs=qk[:, h, i * 128:T],
                                     start=True, stop=True)], reads=qkr, writes=[pr])
                    S.op("dve", I("tensor_tensor", out=sw[:, offs[i]:offs[i] + 128], in0=pt[:, 0:128], in1=amask[:, 128:256], op=ALU.mult),
                         reads=[pr, R_const], writes=[swr])
                    if n > 128:
                        S.op("act", I("activation", out=sw[:, offs[i] + 128:offs[i] + n], in_=pt[:, 128:n], func=AF.Copy),
                             reads=[pr], writes=[swr])

            def phB(h):
                sw, swr = SwTb[h % 2]
                for j in range(NB):
                    pn, pnr = psum()
                    fns = [I("matmul", pn[:, 0:DVX], lhsT=sw[:, offs[i] + (j - i) * 128:offs[i] + (j - i + 1) * 128], rhs=Vx[:, i, h, :],
                             start=(i == 0), stop=False) for i in range(j + 1)]
                    fns.append(I("matmul", pn[:, 0:DVX], lhsT=qk[:, h, j * 128:(j + 1) * 128], rhs=Cb[:, h, :], start=False, stop=True))
                    S.group("pe", fns, reads=[swr, R_Cb] + qkr + [R_Vx[i] for i in range(j + 1)], writes=[pnr])
                    head_out(pn, pnr, j, h, 128, otm, otr)
            pipeline(NH, [phA, phB], lag=1)
        for h in range(NH):
            pc, pcr = psum()
            fns = [I("matmul", pc[:, 0:DVX], lhsT=k_tm[:, i, h * DK:(h + 1) * DK], rhs=Vx[:, i, h, :], start=(i == 0), stop=(i == NB - 1))
                   for i in range(NB)]
            S.group("pe", fns, reads=[R_ktm[i] for i in range(NB)] + [R_Vx[i] for i in range(NB)], writes=[pcr])
            S.op("dve", I("tensor_tensor", out=Cst[:, h, :], in0=pc[:, 0:DVX], in1=Cst[:, h, :], op=ALU.add),
                 reads=[pcr, R_Cst], writes=[R_Cst])

    def fence(old, new):
        S.op("dve", I("memset", fdummy[:, :], 0.0), writes=list(old) + list(new) + [R_fd])

    def mlstm_core_sample(qk, qkr, otm, otr):
        P = slice(0, ST)
        arena = hT[:, :, :].rearrange("p a b -> p (a b)")
        off = [0]

        def carve(n, parts=128):
            v = arena[0:parts, off[0]:off[0] + n]
            off[0] += n + (n % 2)
            return v
        CinR = [(carve(516)[:, 0:514].bitcast(F32), Res(), Cin[k % 2][2] if k < 2 else S.newsem("cix%d" % k)) for k in range(6)]
        CoutR = [(carve(516)[:, 0:514].bitcast(F32), Res(), s_co[k] if k < 2 else S.newsem("cox%d" % k)) for k in range(4)]
        CinbR = [(carve(258)[:, 0:257], Res()) for k in range(3)]
        ZqR = [(carve(496).rearrange("p (h c) -> p h c", c=124), Res()) for k in range(2)]
        KzR = [(carve(512, 64), Res()) for k in range(2)]
        assert off[0] <= 8192
        all_res = [r[1] for r in CinR + CoutR + CinbR + ZqR + KzR]
        fence(R_hT, all_res)
        for zq, zqr in ZqR:
            S.op("dve", I("memset", zq, 0.0), writes=[zqr])
        ptS, prS = psum()
        fns = [I("matmul", ptS[P, h * 64:(h + 1) * 64], lhsT=qk[:, 4 + h, 0:ST], rhs=qk[:, h, 0:ST], start=True, stop=True)
               for h in range(NH)]
        S.group("pe", fns, reads=qkr, writes=[prS])
        S.op("dve", I("tensor_tensor", out=SwT[P, :, :], in0=ptS[P, 0:256].rearrange("p (h t) -> p h t", t=64),
                      in1=smask[:, :].unsqueeze(1).to_broadcast([ST, NH, 64]), op=ALU.mult),
             reads=[prS, R_const], writes=[R_SwT])
        pN = [psum(reserve=True) for _ in range(NH)]
        for h in range(NH):
            pn, pnr = pN[h]
            S.group("pe", [I("matmul", pn[P, 0:DVX], lhsT=SwT[P, h, :], rhs=Vx[P, 0, h, :], start=True, stop=False)],
                    reads=[R_SwT, R_Vx[0]], writes=[pnr])
        live = {}

        def phA(idx):
            i, h = idx // NH, idx % NH
            zq, zqr = ZqR[i % 2]
            kz, kzr = KzR[i % 2]
            if h == 0:
                S.op("dve", I("tensor_copy", out=zq[:, :, 60:64], in_=qk[:, 0:NH, 4 * i:4 * i + 4]), reads=qkr + [zqr], writes=[zqr])
                S.op("dve", I("tensor_scalar", out=kz, in0=k_tm[P, 0, :], scalar1=rowmask[:, i:i + 1], scalar2=None, op0=ALU.mult),
                     reads=[R_ktm[0], R_const], writes=[kzr])
            ci, cir, cis = CinR[idx % 6]
            cb, cbr = CinbR[idx % 3]
            pn, pnr = pN[h]
            S.dma_group("sp", [I("dma_start", out=ci[:, 0:DV], in_=sC_d[i, h, :, :]),
                                I("dma_start", out=ci[:, DV:DVX], in_=sn_d[i, h, :].unsqueeze(1))], writes=[cir], dsem=cis)
            S.op("dve", I("tensor_scalar", out=ci, in0=ci, scalar1=decall[:, h, i:i + 1], scalar2=None, op0=ALU.mult),
                 reads=[cir, R_decall], writes=[cir])
            S.op("act", I("activation", out=cb, in_=ci, func=AF.Copy), reads=[cir], writes=[cbr])
            S.group("pe", [I("matmul", pn[P, 0:DVX], lhsT=zq[:, h, 60 - 4 * i:124 - 4 * i], rhs=cb, start=False, stop=(i == NS - 1))],
                    reads=[zqr, cbr, pnr], writes=[pnr])
            pc, pcr = psum()
            S.group("pe", [I("matmul", pc[:, 0:DVX], lhsT=kz[:, h * DK:(h + 1) * DK], rhs=Vx[P, 0, h, :], start=True, stop=True)],
                    reads=[kzr, R_Vx[0]], writes=[pcr])
            live[idx] = (pc, pcr)

        def phB(idx):
            i, h = idx // NH, idx % NH
            ci, cir, cis = CinR[idx % 6]
            co, cor, cos = CoutR[idx % 4]
            pc, pcr = live.pop(idx)
            S.op("dve", I("tensor_tensor", out=co, in0=pc[:, 0:DVX], in1=ci, op=ALU.add), reads=[pcr, cir], writes=[cor])
            S.op("pool", I("tensor_copy", out=nout[:, idx:idx + 1], in_=co[:, DV:DVX]), reads=[cor], writes=[R_nout])
            S.op("pool", I("dma_start", out=oC_d[i, h, :, :], in_=co[:, 0:DV]), reads=[cor], dsem=cos)
        pipeline(NS * NH, [phA, phB], lag=2)
        ptn, prn = psum()
        S.group("pe", [I("transpose", out=ptn[0:NS * NH, 0:128], in_=nout[:, :], identity=ident_f[:, :])], reads=[R_nout, R_const], writes=[prn])
        S.op("dve", I("tensor_copy", out=noutT[:, :], in_=ptn[0:NS * NH, 0:128]), reads=[prn], writes=[R_noutT])
        S.op("sp", I("dma_start", out=on_d.rearrange("i h d -> (i h) d"), in_=noutT[:, :]), reads=[R_noutT], dsem=s_st)
        for h in range(NH):
            head_out(pN[h][0], pN[h][1], 0, h, ST, otm, otr)
            psum_release(pN[h][1])
        S.op("sp", I("dma_start", out=omT_d[:, :], in_=s_mo[:, :]), reads=[R_smo], dsem=s_st)
        fence(all_res, R_hT)
        samp_sems.extend([c[2] for c in CoutR])

    def kv_stage(NB, PTK, wv, wr, last_store, pre=None):
        S.tag = S.tile + ':kv'
        P = slice(0, PTK)
        fm, fmr = pre if pre is not None else norm_T(NB, PTK, 2)

        def phA(b):
            kf, R_kf = kfR[b % 2]
            kn, R_kn = knR[b % 2]
            sc = small[P, 32 + 4 * (b % 2):36 + 4 * (b % 2)]
            pt, pr = psum()
            fns = [I("matmul", pt[P, :], lhsT=fm[:, k, b * PTK:(b + 1) * PTK], rhs=wv[:, k, 0:512], start=(k == 0), stop=(k == 7))
                   for k in range(8)]
            S.group("pe", fns, reads=[fmr[b], wr], writes=[pr])
            S.op("act", I("activation", out=kf[P, :], in_=pt[P, :], func=AF.Copy), reads=[pr], writes=[R_kf])
            S.op("dve", I("tensor_tensor", out=kn[P, :], in0=kf[P, 0:256], in1=kf[P, 0:256], op=ALU.mult), reads=[R_kf], writes=[R_kn])
            S.op("dve", I("tensor_reduce", out=sc, in_=kn[P, :].rearrange("p (g d) -> p g d", d=64), axis=AX.X, op=ALU.add),
                 reads=[R_kn], writes=[R_smk[b % 2]])

        def phB(b):
            kf, R_kf = kfR[b % 2]
            kn, R_kn = knR[b % 2]
            Ktm, R_Ktm = KtmR[b % 2]
            sc = small[P, 32 + 4 * (b % 2):36 + 4 * (b % 2)]
            rsqrt_small(sc, sc, 4, 1.0 / 64, [R_smk[b % 2]])
            S.op("dve", I("tensor_tensor", out=kn[P, :].rearrange("p (g d) -> p g d", d=64),
                          in0=kf[P, 0:256].rearrange("p (g d) -> p g d", d=64),
                          in1=sc.unsqueeze(2).to_broadcast([PTK, 4, 64]), op=ALU.mult),
                 reads=[R_kf, R_smk[b % 2], R_kn], writes=[R_kn])
            S.op("dve", I("tensor_tensor", out=kn[P, :].rearrange("p (g d) -> p g d", d=64),
                          in0=kn[P, :].rearrange("p (g d) -> p g d", d=64),
                          in1=gk_b[P, :].unsqueeze(1).to_broadcast([PTK, 4, 64]), op=ALU.mult),
                 reads=[R_kn, R_const], writes=[R_kn])
            S.op("act", I("activation", out=Ktm[P, :], in_=kn[P, :], func=AF.Copy), reads=[R_kn], writes=[R_Ktm])
            S.op("act", I("activation", out=Vex[P, b + 1, :, 0:64], in_=kf[P, 256:512].rearrange("p (g d) -> p g d", d=64), func=AF.Copy),
                 reads=[R_kf], writes=[R_Vex[b + 1]])
            if last_store is not None and b == NB - 1:
                last_store(kn, R_kn, kf, R_kf)
            ptt, prt = psum()
            ptb = ptt.bitcast(BF16)
            fns = [I("transpose", out=ptb[:, gp * PTK:(gp + 1) * PTK], in_=Ktm[P, gp * 128:(gp + 1) * 128], identity=ident_b[P, P])
                   for gp in range(2)]
            S.group("pe", fns, reads=[R_Ktm, R_const], writes=[prt])
            S.op("dve", I("tensor_copy", out=kT[:, :, (b + 1) * 128:(b + 1) * 128 + PTK],
                          in_=ptb[:, 0:2 * PTK].rearrange("p (g t) -> p g t", t=PTK)), reads=[prt], writes=[R_kT[b + 1]])
        pipeline(NB, [phA, phB], lag=1)

    def carry_prev(NB, use_flag):
        S.op("dve", I("tensor_copy", out=kT[:, :, 0:128], in_=kT[:, :, NB * 128:(NB + 1) * 128]),
             reads=[R_kT[NB], R_kT[0]], writes=[R_kT[0]])
        if use_flag:
            S.op("dve", I("tensor_scalar", out=Vex[:, 0, :, :], in0=Vex[:, NB, :, :], scalar1=flag[:, 0:1], scalar2=None,
                                                  op0=ALU.mult), reads=[R_Vex[NB], R_const, R_Vex[0]], writes=[R_Vex[0]])
        else:
            S.op("dve", I("tensor_copy", out=Vex[:, 0, :, :], in_=Vex[:, NB, :, :]), reads=[R_Vex[NB], R_Vex[0]], writes=[R_Vex[0]])

    def q_stage(NB, PTK, wv, wr, pre=None):
        S.tag = S.tile + ':q'
        P = slice(0, PTK)
        fm, fmr = pre if pre is not None else norm_T(NB, PTK, 3)
        qtm, qtr = next_tm()
        qT, qTr = next_fm()

        def phA(u):
            b, gp2 = u // 2, u % 2
            qf, R_qf = qfR[u % 2]
            sq, R_sq = kfR[u % 2]
            sc = small[P, 40 + 8 * (u % 2):48 + 8 * (u % 2)]
            pt, pr = psum()
            fns = [I("matmul", pt[P, :], lhsT=fm[:, k, b * PTK:(b + 1) * PTK],
                     rhs=wv[:, k, gp2 * 512:(gp2 + 1) * 512], start=(k == 0), stop=(k == 7)) for k in range(8)]
            S.group("pe", fns, reads=[fmr[b], wr], writes=[pr])
            S.op("act", I("activation", out=qf[P, :], in_=pt[P, :], func=AF.Copy), reads=[pr], writes=[R_qf])
            S.op("dve", I("tensor_tensor", out=sq[P, :], in0=qf[P, :], in1=qf[P, :], op=ALU.mult), reads=[R_qf], writes=[R_sq])
            S.op("dve", I("tensor_reduce", out=sc, in_=sq[P, :].rearrange("p (h d) -> p h d", d=64), axis=AX.X, op=ALU.add),
                 reads=[R_sq], writes=[R_smq[u % 2]])

        def phB(u):
            b, gp2 = u // 2, u % 2
            qf, R_qf = qfR[u % 2]
            sc = small[P, 40 + 8 * (u % 2):48 + 8 * (u % 2)]
            rsqrt_small(sc, sc, 8, 1.0 / 64, [R_smq[u % 2]])
            S.op("dve", I("tensor_tensor", out=qf[P, :].rearrange("p (h d) -> p h d", d=64),
                          in0=qf[P, :].rearrange("p (h d) -> p h d", d=64),
                          in1=gq_b[P, :].unsqueeze(1).to_broadcast([PTK, 8, 64]), op=ALU.mult),
                 reads=[R_qf, R_const], writes=[R_qf])
            src = qf[P, :].rearrange("p (e j d) -> p e j d", e=2, j=4)
            rs = sc.rearrange("p (e j) -> p e j", e=2).unsqueeze(3).to_broadcast([PTK, 2, 4, 64])
            dst = qtm[P, b, gp2 * 512:(gp2 + 1) * 512].rearrange("p (j e d) -> p e j d", e=2, j=4)
            S.op("dve", I("tensor_tensor", out=dst, in0=src, in1=rs, op=ALU.mult), reads=[R_qf, R_smq[u % 2]], writes=[qtr[b]])
            if gp2 == 1:
                tm_to_fm_blk(qtm, qtr, qT, qTr, b, PTK, None, use_act=(b % 2 == 0))
        pipeline(NB * 2, [phA, phB], lag=1)
        return qT, qTr

    def attn_prompt(NB, qT, qTr, wv_o, wr_o_get):
        S.tag = S.tile + ':attn'
        otm, otr = next_tm()

        def phA(u):
            b, g = u // 4, u % 4
            base = (g % 2) * 64
            pb = slice(base, base + 64)
            P0 = (g // 2) * 4
            ptile, ptr_ = PT[u % 2]
            for kb in range(2):
                slot = b + kb
                pt, pr = psum()
                S.group("pe", [I("matmul", pt[:, :], lhsT=kT[pb, g // 2, slot * 128:(slot + 1) * 128],
                                 rhs=qT[pb, P0:P0 + 4, b * 128:(b + 1) * 128], start=True, stop=True)],
                        reads=[R_kT[slot], qTr[b]], writes=[pr])
                S.op("act", I("activation", out=ptile[:, kb, :], in_=pt[:, :], func=AF.Exp, scale=0.125), reads=[pr], writes=[ptr_])
                S.op("dve", I("tensor_tensor", out=ptile[:, kb, :].rearrange("p (j q) -> p j q", q=128),
                              in0=ptile[:, kb, :].rearrange("p (j q) -> p j q", q=128),
                              in1=amask[:, kb * 128:(kb + 1) * 128].unsqueeze(1).to_broadcast([128, 4, 128]), op=ALU.mult),
                     reads=[ptr_, R_const], writes=[ptr_])

        def phB(u):
            b, g = u // 4, u % 4
            ptile, ptr_ = PT[u % 2]
            po, por = psum()
            fns = []
            for j in range(4):
                for kb in range(2):
                    fns.append(I("matmul", po[:, j * 65:(j + 1) * 65], lhsT=ptile[:, kb, j * 128:(j + 1) * 128],
                                 rhs=Vex[:, b + kb, g, :], start=(kb == 0), stop=(kb == 1)))
            S.group("pe", fns, reads=[ptr_, R_Vex[b], R_Vex[b + 1]], writes=[por])
            attn_finish(po, por, 128, b, g, otm, otr)
        pipeline(NB * 4, [phA, phB], lag=1)
        fo, fo_r = next_fm()
        tm_to_fm(otm, otr, fo, fo_r, NB, 128, None)
        return fo, fo_r

    def attn_finish(po, por, PTK, b, g, otm, otr):
        P = slice(0, PTK)
        po3 = po[P, 0:260].rearrange("p (j d) -> p j d", d=65)
        S.op("dve", I("tensor_tensor", out=small[P, 56:60], in0=po3[:, :, 64:65].rearrange("p j d -> p (j d)"),
                                              in1=sink_b[P, g * 4:(g + 1) * 4], op=ALU.add),
             reads=[por, R_const], writes=[R_sma])
        S.op("dve", I("reciprocal", out=small[P, 56:60], in_=small[P, 56:60]), reads=[R_sma], writes=[R_sma])
        S.op("dve", I("tensor_tensor", out=otm[P, b, g * 256:(g + 1) * 256].rearrange("p (j d) -> p j d", d=64),
                                              in0=po3[:, :, 0:64], in1=small[P, 56:60].unsqueeze(2).to_broadcast([PTK, 4, 64]), op=ALU.mult),
             reads=[por, R_sma], writes=[otr[b]])

    def attn_sample(qT, qTr):
        S.tag = S.tile + ':attn'
        P = slice(0, ST)
        otm, otr = next_tm()
        cvb = hT[:, :, :].rearrange("p a b -> p (a b)")[:, 0:4160].rearrange("p (i g d) -> p i g d", g=4, d=65)
        fence(R_hT, [R_cvb])
        S.op("dve", I("memset", cvb, 1.0), writes=[R_cvb])
        ckb = xres[:, 1:3, :].bitcast(BF16).rearrange("p a (i c) -> p (a i) c", c=256)
        R_ckb = [R_x[1], R_x[2]]
        S.op("pool", I("dma_start", out=ckb, in_=cwk_d.rearrange("i w c -> w i c")), writes=R_ckb, dsem=s_ck)
        S.dma_group("pool", [I("dma_start", out=cvb[:, :, g_, 0:64], in_=cwv_d[:, :, g_ * 64:(g_ + 1) * 64].rearrange("i w d -> w i d"))
                             for g_ in range(4)], reads=[R_cvb], writes=[R_cvb], dsem=s_cv)
        for g in range(4):
            base = (g % 2) * 64
            pb = slice(base, base + 64)
            P0 = (g // 2) * 4
            if g % 2 == 0:
                for i in range(NS):
                    ptt, prt = psum()
                    ptb = ptt.bitcast(BF16)
                    S.group("pe", [I("transpose", out=ptb[:, 0:128], in_=ckb[:, i, (g // 2) * 128:(g // 2 + 1) * 128], identity=ident_b[:, :])],
                            reads=R_ckb + [R_const], writes=[prt])
                    evac(i, KcT[:, i, :], ptb[:, 0:128], [prt], [R_KcT])
            pt, pr = psum()
            S.group("pe", [I("matmul", pt[P, 0:256], lhsT=kT[pb, g // 2, 128:128 + ST], rhs=qT[pb, P0:P0 + 4, 0:ST],
                                              start=True, stop=True)],
                    reads=[R_kT[1], qTr[0]], writes=[pr])
            S.op("act", I("activation", out=PTn[:, :, :].rearrange("p j q -> p (j q)"), in_=pt[P, 0:256], func=AF.Exp, scale=0.125),
                 reads=[pr], writes=[R_PTn])
            S.op("dve", I("tensor_tensor", out=PTn[:, :, :], in0=PTn[:, :, :], in1=smask[:, :].unsqueeze(1).to_broadcast([ST, 4, ST]),
                                                  op=ALU.mult), reads=[R_PTn, R_const], writes=[R_PTn])
            for i in range(NS):
                pt2, pr2 = psum()
                S.group("pe", [I("matmul", pt2[:, 0:16], lhsT=KcT[pb, i, :], rhs=qT[pb, P0:P0 + 4, 4 * i:4 * i + 4],
                                                                  start=True, stop=True)],
                        reads=[R_KcT, qTr[0]], writes=[pr2])
                S.op("act", I("activation", out=PTc[:, i, :, 4 * i:4 * i + 4],
                                                                   in_=pt2[:, 0:16].rearrange("p (j t) -> p j t", t=4), func=AF.Exp, scale=0.125),
                     reads=[pr2], writes=[R_PTc[i]])
                S.op("dve", I("tensor_tensor", out=PTc[:, i, :, 4 * i:4 * i + 4], in0=PTc[:, i, :, 4 * i:4 * i + 4],
                                                                 in1=cmask[:, :].unsqueeze(1).to_broadcast([128, 4, 4]), op=ALU.mult),
                     reads=[R_PTc[i], R_const], writes=[R_PTc[i]])
            po, por = psum()
            fns = []
            for j in range(4):
                for i in range(NS):
                    fns.append(I("matmul", po[P, j * 65:(j + 1) * 65], lhsT=PTc[:, i, j, :], rhs=cvb[:, i, g, :],
                                                                start=(i == 0), stop=False))
                fns.append(I("matmul", po[P, j * 65:(j + 1) * 65], lhsT=PTn[:, j, :], rhs=Vex[P, 1, g, :],
                                                         start=False, stop=True))
            S.group("pe", fns, reads=R_PTc + [R_PTn, R_cvb, R_Vex[1]], writes=[por])
            attn_finish(po, por, ST, 0, g, otm, otr)
        fo, fo_r = next_fm()
        tm_to_fm(otm, otr, fo, fo_r, 1, ST, None)
        fence([R_cvb], R_hT)
        return fo, fo_r

    def load_x(src_rows, NB, PTK):
        for b in range(NB):
            S.op("sp", I("dma_start", out=xres[0:PTK, b, :], in_=src_rows[b * PTK:(b + 1) * PTK, :]),
                 writes=[R_x[b]], dsem=s_ldx[b])

    def plan_tile(kind, mode="f32"):
        mlp_chunks = ("w1_0", "w1_1", "w2_0", "w1_2", "w2_1", "w1_3", "w2_2", "w2_3")
        if kind == "warm":
            for k in ("inB", "inQK", "inV", "out") + mlp_chunks:
                plan_chunk(k, 0, mode)
            plan_chunk("kv", 0, mode)
        else:
            for k in ("inB", "inQK", "inV", "out") + mlp_chunks:
                plan_chunk(k, 0, mode)
            plan_chunk("kv", 0, mode); plan_chunk("q", 0, mode); plan_chunk("o", 0, mode)
            for k in mlp_chunks:
                plan_chunk(k, 1, mode)

    pre_tiles = [(0, 4), (512, 4), (1024, 4), (1536, 3)]
    for k_ in range(len(pre_tiles)):
        plan_chunk("inB", 0, "part"); plan_chunk("inQK", 0, "part")
        if k_ >= 1:
            plan_chunk("inV", 0, "part")
    plan_chunk("inV", 0, "part")
    plan_tile("warm")
    for t_ in range(4):
        plan_tile("main", "f32wb" if t_ == 3 else "f32")
    plan_tile("main", "sc")

    pipeline(len(pre_tiles), [lambda k: prefix_A(k, *pre_tiles[k]), lambda k: prefix_B(k, *pre_tiles[k])], lag=1)
    sel(0)
    S.tile = 'warm'
    load_x(xpre[1920:2048, :], 1, 128)
    mlstm(1, 128, True, False)
    mlp(1, 128, 0)
    wv, wr = wget("kv")
    kv_stage(1, 128, wv, wr, None)
    wdone()
    carry_prev(1, True)
    S.op("dve", I("tensor_scalar", out=Cst[:, :, :], in0=Cst[:, :, :], scalar1=flag[:, 0:1], scalar2=None, op0=ALU.mult),
         reads=[R_Cst, R_const], writes=[R_Cst])
    S.op("dve", I("tensor_scalar", out=carryA[:, :], in0=carryA[:, :], scalar1=flag[0:4, 0:1], scalar2=None, op0=ALU.mult),
         reads=[R_cA, R_const], writes=[R_cA])
    S.op("dve", I("tensor_scalar", out=carryM[:, :], in0=carryM[:, :], scalar1=flag[0:4, 0:1], scalar2=None, op0=ALU.mult),
         reads=[R_cM, R_const], writes=[R_cM])

    for t in range(4):
        r0 = t * 512
        S.tile = 'main%d' % t
        selx(t % 2)
        if t == 0:
            load_x(xp[r0:r0 + 512, :], 4, 128)
            ml_pre = None
        mlstm(4, 128, True, False, pre=ml_pre)
        mlp(4, 128, 0)
        S.tag = S.tile + ':kv'
        fkv, fkvr, fq, fqr = norm_T2(4, 128, 2, 3)
        wv, wr = wget("kv")
        ls = None
        if t == 3:
            def ls(kn_, rkn, kf_, rkf):
                S.op("sp", I("dma_start", out=pwk_d[:, :], in_=kn_[:, :]), reads=[rkn], dsem=s_st)
                S.op("sp", I("dma_start", out=pwv_d[:, :], in_=kf_[:, 256:512]), reads=[rkf], dsem=s_st)
        kv_stage(4, 128, wv, wr, ls, pre=(fkv, fkvr))
        wdone()
        wv, wr = wget("q")
        qT, qTr = q_stage(4, 128, wv, wr, pre=(fq, fqr))
        wdone()
        fo, fo_r = attn_prompt(4, qT, qTr, None, None)
        carry_prev(4, False)
        S.tag = S.tile + ':wo'
        wv, wr = wget("o")
        resid_add(4, 128, fo, fo_r, wv, wr, 8)
        wdone()
        if t < 3:
            selx((t + 1) % 2)
            load_x(xp[r0 + 512:r0 + 1024, :], 4, 128)
            selx(t % 2)
            nxt = {}

            def hook(t=t):
                tag = (S.tile, S.tag)
                selx((t + 1) % 2)
                S.tile = 'main%d' % (t + 1)
                nxt['pre'] = norm_T(4, 128, 0)
                selx(t % 2)
                S.tile, S.tag = tag
            mlp(4, 128, 1, hook=hook)
            ml_pre = nxt['pre']
        else:
            mlp(4, 128, 1)
        for b in range(4):
            S.op("sp", I("dma_start", out=y_d[r0 + b * 128:r0 + (b + 1) * 128, :], in_=xres[:, b, :]),
                 reads=[R_x[b]], dsem=s_sty[b])
    for h in range(NH):
        S.op("sp", I("dma_start", out=pC_d[h, :, :], in_=Cst[:, h, 0:DV]), reads=[R_Cst], dsem=s_st)
        S.op("sp", I("dma_start", out=pn_d[h, :].unsqueeze(1), in_=Cst[:, h, DV:DVX]), reads=[R_Cst], dsem=s_st)
    mfin = grow["t2"][0]
    S.op("dve", I("tensor_tensor", out=mfin[:, 0:1], in0=carryM[:, 0:1], in1=carryA[:, 0:1], op=ALU.subtract),
         reads=[R_cM, R_cA, grow["t2"][1]], writes=[grow["t2"][1]])
    S.op("sp", I("dma_start", out=pm_d[:, :], in_=mfin[:, 0:1]), reads=[grow["t2"][1]], dsem=s_st)

    S.tile = 'samp'
    selx(0)
    load_x(xs_d, 1, ST)
    mlstm(1, ST, True, True)
    mlp(1, ST, 0)
    wv, wr = wget("kv")

    def ls_s(kn_, rkn, kf_, rkf):
        for i in range(NS):
            S.op("sp", I("dma_start", out=owk_d[i, 124:128, :], in_=kn_[4 * i:4 * i + 4, :]), reads=[rkn], dsem=s_st)
            S.op("sp", I("dma_start", out=owv_d[i, 124:128, :], in_=kf_[4 * i:4 * i + 4, 256:512]), reads=[rkf], dsem=s_st)
    kv_stage(1, ST, wv, wr, ls_s)
    wdone()
    wv, wr = wget("q")
    qT, qTr = q_stage(1, ST, wv, wr)
    wdone()
    fo, fo_r = attn_sample(qT, qTr)
    wv, wr = wget("o")
    resid_add(1, ST, fo, fo_r, wv, wr, 8)
    wdone()
    mlp(1, ST, 1)
    S.op("sp", I("dma_start", out=ys_d[:, :], in_=xres[0:ST, 0, :]), reads=[R_x[0]], dsem=s_sty[0])
    S.op("sp", I("dma_start", out=owk_d[:, 0:124, :], in_=cwk_d[:, 4:128, :]), dsem=s_st)
    S.op("sp", I("dma_start", out=owv_d[:, 0:124, :], in_=cwv_d[:, 4:128, :]), dsem=s_st)

    _NC_CACHE['sbuf_free'] = nc.sbuf_bytes_remaining
    final_waits = [(k, S.cnt[k]) for k in list(dict.fromkeys([s_st] + s_sty + s_co + samp_sems)) if S.cnt[k] > 0]
    with nc.Block() as block:
        S.emit(block, final_waits)
    es.close()
    return nc


_NC_CACHE = {}


def _consts():
    c = {}
    c["ident_b"] = np.eye(128, dtype=np.float32).astype(ml_dtypes.bfloat16)
    c["ident_f"] = np.eye(128, dtype=np.float32)
    p = np.arange(128)[:, None]
    t = np.arange(64)[None, :]
    c["mlmask"] = ((p % 64) <= t).astype(np.float32).astype(ml_dtypes.bfloat16)
    ps = np.arange(64)[:, None]
    c["smask"] = (((ps // 4) == (t // 4)) & (ps <= t)).astype(np.float32).astype(ml_dtypes.bfloat16)
    q = np.arange(128)[None, :]
    am = np.concatenate([(p >= q), (p <= q)], axis=1)
    c["amask"] = am.astype(np.float32).astype(ml_dtypes.bfloat16)
    c["cmask"] = (p >= np.arange(4)[None, :]).astype(np.float32).astype(ml_dtypes.bfloat16)
    c["rowmask"] = ((ps // 4) == np.arange(16)[None, :]).astype(np.float32)
    c["eye4"] = np.eye(4, dtype=np.float32)
    c["ones4"] = np.ones((4, 128), np.float32)
    return c


def kernel(x_prompt, x_sample, state_mlstm_C, state_mlstm_n, state_mlstm_m, cache_win_k, cache_win_v,
           ml_norm_g, ml_w_in, ml_b_i, ml_b_f, ml_head_g, ml_w_out,
           kv_norm_g, w_kv, k_norm_g,
           att_norm_g, att_w_q, q_norm_g, att_sinks, att_w_o,
           mlp_norm_g, mlp_w1, mlp_w2):
    f = lambda a: np.ascontiguousarray(np.asarray(a, dtype=np.float32))
    x_prompt = f(x_prompt); x_sample = f(x_sample)
    sC = f(state_mlstm_C); sn = f(state_mlstm_n); sm = f(state_mlstm_m)
    cwk = f(cache_win_k); cwv = f(cache_win_v)
    if "nc" not in _NC_CACHE:
        _NC_CACHE["nc"] = build_program()
    nc = _NC_CACHE["nc"]

    def col(g):
        return f(g).reshape(8, 128).T
    gcols = np.ascontiguousarray(np.concatenate(
        [col(ml_norm_g[0]), col(mlp_norm_g[0]), col(kv_norm_g), col(att_norm_g[0]), col(mlp_norm_g[1]), col(ml_head_g[0])], axis=1))
    shared = dict(
        w_in=f(ml_w_in[0]), w_out=f(ml_w_out[0]), w1=f(mlp_w1), w2=f(mlp_w2), w_kv=f(w_kv), w_q=f(att_w_q[0]), w_o=f(att_w_o[0]),
        gcols=gcols, bif=np.ascontiguousarray(np.stack([f(ml_b_i[0]), f(ml_b_f[0])], axis=1)),
        gk_b=np.ascontiguousarray(np.broadcast_to(f(k_norm_g)[None, :], (128, 64))),
        gq_b=np.ascontiguousarray(np.broadcast_to(f(q_norm_g[0])[None, :], (128, 64))),
        sink_b=np.ascontiguousarray(np.broadcast_to(f(att_sinks[0])[None, :], (128, 16))),
    )
    shared.update(_consts())
    in_maps = []
    for c in range(8):
        b, hh = c // 2, c % 2
        m = dict(shared)
        m["xp"] = np.ascontiguousarray(x_prompt[b, hh * HALF:(hh + 1) * HALF])
        m["xpre"] = np.ascontiguousarray(x_prompt[b, 0:HALF]) if hh == 1 else np.zeros((HALF, D), np.float32)
        m["flag"] = np.full((128, 1), float(hh), np.float32)
        sl = slice(c * NS, (c + 1) * NS)
        m["xs"] = np.ascontiguousarray(x_sample[sl].reshape(ST, D))
        m["sC"] = np.ascontiguousarray(sC[0, sl])
        m["sn"] = np.ascontiguousarray(sn[0, sl])
        m["smT"] = np.ascontiguousarray(sm[0, sl].T)
        m["cwk"] = np.ascontiguousarray(cwk[sl].reshape(NS, 128, 256))
        m["cwv"] = np.ascontiguousarray(cwv[sl].reshape(NS, 128, 256))
        in_maps.append(m)
    res = run_bass_kernel_spmd(nc, in_maps, core_ids=list(range(8)))
    R = res.results
    B = 4
    y_prompt = np.zeros((B, 4096, D), np.float32)
    y_sample = np.zeros((128, 4, D), np.float32)
    p_C = np.zeros((1, B, NH, DK, DV), np.float32); p_n = np.zeros((1, B, NH, DK), np.float32); p_m = np.zeros((1, B, NH), np.float32)
    p_wk = np.zeros((B, 128, 4, 64), np.float32); p_wv = np.zeros((B, 128, 4, 64), np.float32)
    s_C = np.zeros((1, 128, NH, DK, DV), np.float32); s_n = np.zeros((1, 128, NH, DK), np.float32); s_m = np.zeros((1, 128, NH), np.float32)
    s_wk = np.zeros((128, 128, 4, 64), np.float32); s_wv = np.zeros((128, 128, 4, 64), np.float32)
    for c in range(8):
        b, hh = c // 2, c % 2
        r = R[c]
        y_prompt[b, hh * HALF:(hh + 1) * HALF] = r["y"]
        sl = slice(c * NS, (c + 1) * NS)
        y_sample[sl] = r["ys"].reshape(NS, 4, D)
        if hh == 1:
            p_C[0, b] = r["pC"]; p_n[0, b] = r["pn"]; p_m[0, b] = r["pm"].reshape(NH)
            p_wk[b] = r["pwk"].reshape(128, 4, 64); p_wv[b] = r["pwv"].reshape(128, 4, 64)
        s_C[0, sl] = r["oC"]; s_n[0, sl] = r["on"]; s_m[0, sl] = r["omT"].T
        s_wk[sl] = r["owk"].reshape(NS, 128, 4, 64); s_wv[sl] = r["owv"].reshape(NS, 128, 4, 64)
    return (y_prompt, y_sample, p_C, p_n, p_m, p_wk, p_wv, s_C, s_n, s_m, s_wk, s_wv)
```

```python
import numpy as np
import ml_dtypes
from contextlib import ExitStack
import concourse.bass as bass
import concourse.mybir as mybir
from concourse.bass_utils import run_bass_kernel_spmd

F32 = mybir.dt.float32
BF16 = mybir.dt.bfloat16
AF = mybir.ActivationFunctionType
ALU = mybir.AluOpType
AX = mybir.AxisListType

D = 1024
DFF = 4096
NH = 4
DK = 128
DV = 256
DVX = 257
EPS = 1e-6
NS = 16
ST = 64
HALF = 2048
SLOT = 8448
NSLOT = 3


def I(meth, *a, **k):
    return lambda e: getattr(e, meth)(*a, **k)


class Res:
    __slots__ = ("w", "r", "name")

    def __init__(self, name=""):
        self.w = None
        self.r = []
        self.name = name


class Sched:
    ENG = ("pe", "act", "dve", "pool", "sp")

    def __init__(self, nc, es):
        self.nc = nc
        self.es = es
        self.q = {e: [] for e in self.ENG}
        self.sems = {}
        self.cnt = {}
        self.seen = {e: {} for e in self.ENG}
        self.tag = ''
        self.tile = 'init'
        self.tagmap = {}
        for e in self.ENG:
            self.newsem(e)

    def newsem(self, key):
        self.sems[key] = self.es.enter_context(self.nc.semaphore("s_" + key))
        self.cnt[key] = 0
        return key

    def _waits(self, eng, reads, writes):
        waits = {}

        def need(dep):
            if dep is None:
                return
            k, v = dep
            if self.seen[eng].get(k, 0) >= v:
                return
            if waits.get(k, 0) < v:
                waits[k] = v
        for r in reads:
            need(r.w)
        for w in writes:
            need(w.w)
            for d in w.r:
                need(d)
        for k, v in waits.items():
            self.seen[eng][k] = v
        return list(waits.items())

    def op(self, eng, fn, reads=(), writes=(), dsem=None):
        waits = self._waits(eng, reads, writes)
        if dsem is not None:
            self.cnt[dsem] += 16
            me = (dsem, self.cnt[dsem])
            self.q[eng].append((waits, fn, (dsem, 16), self.tag))
        else:
            self.cnt[eng] += 1
            me = (eng, self.cnt[eng])
            self.q[eng].append((waits, fn, (eng, 1), self.tag))
        for r in reads:
            r.r.append(me)
        for w in writes:
            w.w = me
            w.r = []
        return me

    def dma_group(self, eng, fns, reads=(), writes=(), dsem=None):
        waits = self._waits(eng, reads, writes)
        for i, fn in enumerate(fns):
            self.cnt[dsem] += 16
            self.q[eng].append((waits if i == 0 else [], fn, (dsem, 16), self.tag))
        me = (dsem, self.cnt[dsem])
        for r in reads:
            r.r.append(me)
        for w in writes:
            w.w = me
            w.r = []
        return me

    def group(self, eng, fns, reads=(), writes=()):
        waits = self._waits(eng, reads, writes)
        self.cnt[eng] += 1
        me = (eng, self.cnt[eng])
        n = len(fns)
        for i, fn in enumerate(fns):
            self.q[eng].append((waits if i == 0 else [], fn, (eng, 1) if i == n - 1 else None, self.tag))
        for r in reads:
            r.r.append(me)
        for w in writes:
            w.w = me
            w.r = []
        return me

    def emit(self, block, final_waits):
        nc = self.nc
        sems = self.sems

        def run(eng_key):
            def body(e):
                for waits, fn, inc, tag in self.q[eng_key]:
                    for k, v in waits:
                        e.wait_ge(sems[k], v)
                    ins = fn(e)
                    try:
                        self.tagmap[ins.ins.name] = tag
                    except Exception:
                        pass
                    if inc is not None:
                        ins.then_inc(sems[inc[0]], inc[1])
                if eng_key == "sp":
                    for k, v in final_waits:
                        e.wait_ge(sems[k], v)
            return body
        block.tensor(run("pe"))
        block.scalar(run("act"))
        block.vector(run("dve"))
        block.gpsimd(run("pool"))
        block.sync(run("sp"))


def build_program():
    nc = bass.Bass("TRN2", target_bir_lowering=False)
    es = ExitStack()
    S = Sched(nc, es)
    _NC_CACHE['S'] = S

    def din(name, shape, dt=F32):
        return nc.dram_tensor(name, list(shape), dt, kind="ExternalInput").ap()

    def dout(name, shape, dt=F32):
        return nc.dram_tensor(name, list(shape), dt, kind="ExternalOutput").ap()

    xp = din("xp", [HALF, D])
    xpre = din("xpre", [HALF, D])
    flag_d = din("flag", [128, 1])
    xs_d = din("xs", [ST, D])
    sC_d = din("sC", [NS, NH, DK, DV])
    sn_d = din("sn", [NS, NH, DK])
    smT_d = din("smT", [NH, NS])
    cwk_d = din("cwk", [NS, 128, 256])
    cwv_d = din("cwv", [NS, 128, 256])
    w_in_d = din("w_in", [D, 3080])
    w_out_d = din("w_out", [D, D])
    w1_d = din("w1", [2, D, DFF])
    w2_d = din("w2", [2, DFF, D])
    w_kv_d = din("w_kv", [D, 512])
    w_q_d = din("w_q", [D, D])
    w_o_d = din("w_o", [D, D])
    gcols_d = din("gcols", [128, 48])
    bif_d = din("bif", [NH, 2])
    gk_d = din("gk_b", [128, 64])
    gq_d = din("gq_b", [128, 64])
    sink_d = din("sink_b", [128, 16])
    identb_d = din("ident_b", [128, 128], BF16)
    identf_d = din("ident_f", [128, 128])
    mlmask_d = din("mlmask", [128, 64], BF16)
    smask_d = din("smask", [64, 64], BF16)
    amask_d = din("amask", [128, 256], BF16)
    cmask_d = din("cmask", [128, 4], BF16)
    rowmask_d = din("rowmask", [64, 16])
    eye4_d = din("eye4", [4, 4])
    ones4_d = din("ones4", [4, 128])

    def dscr(name, shape):
        return nc.dram_tensor(name, list(shape), BF16, kind="Internal").ap()
    sc_in = dscr("sc_in", [D, 3080]); sc_out = dscr("sc_out", [D, D])
    sc_w1 = dscr("sc_w1", [2, D, DFF]); sc_w2 = dscr("sc_w2", [2, DFF, D])
    sc_kv = dscr("sc_kv", [D, 512]); sc_q = dscr("sc_q", [D, D]); sc_o = dscr("sc_o", [D, D])

    y_d = dout("y", [HALF, D])
    ys_d = dout("ys", [ST, D])
    pC_d = dout("pC", [NH, DK, DV])
    pn_d = dout("pn", [NH, DK])
    pm_d = dout("pm", [NH, 1])
    pwk_d = dout("pwk", [128, 256])
    pwv_d = dout("pwv", [128, 256])
    oC_d = dout("oC", [NS, NH, DK, DV])
    on_d = dout("on", [NS, NH, DK])
    omT_d = dout("omT", [NH, NS])
    owk_d = dout("owk", [NS, 128, 256])
    owv_d = dout("owv", [NS, 128, 256])

    def sb(name, shape, dt=F32):
        return es.enter_context(nc.sbuf_tensor("sb_" + name, list(shape), dt))

    xbufs = [(sb("xres%d" % i, [128, 4, D]), [Res("x%d_%d" % (i, j)) for j in range(4)]) for i in range(2)]
    xres, R_x = xbufs[0]
    TM = [(sb("tm%d" % i, [128, 4, D], BF16), [Res() for _ in range(4)]) for i in range(2)]
    FM = [(sb("fm%d" % i, [128, 8, 512], BF16), [Res() for _ in range(4)]) for i in range(2)]
    hT = sb("hT", [128, 16, 512], BF16); R_hT = [Res() for _ in range(16)]
    wslot = [(sb("ws%d" % i, [128, SLOT], BF16), Res(), S.newsem("w%d" % i)) for i in range(NSLOT)]
    ktmR = [(sb("k_tm%d" % i, [128, 4, 512], BF16), [Res() for _ in range(4)]) for i in range(2)]
    k_tm, R_ktm = ktmR[0]
    Vx = sb("Vx", [128, 4, NH, DVX], BF16); R_Vx = [Res() for _ in range(4)]
    Cst = sb("Cst", [128, NH, DVX]); R_Cst = Res()
    Cb = sb("Cb", [128, NH, DVX], BF16); R_Cb = Res()
    SwTb = [(sb("SwT%d" % i, [128, 1280], BF16), Res()) for i in range(2)]
    SwT = SwTb[0][0][:, 0:256].rearrange("p (h t) -> p h t", t=64); R_SwT = SwTb[0][1]
    grow = {n: (sb("g_" + n, [4, 512]), Res()) for n in ("ti", "A", "t1", "t2")}
    grow["tf"] = grow["t2"]; grow["gp"] = grow["ti"]
    gsm = {n: (sb("gs_" + n, [4, 16]), Res()) for n in ("cm", "Mq", "Mp", "dec")}
    ddg = sb("ddg", [4, NH, 16]); R_ddg = Res()
    carryA = sb("carryA", [4, 1]); R_cA = Res()
    carryM = sb("carryM", [4, 1]); R_cM = Res()
    ewlbR = [(sb("ewlb%d" % i, [128, 4, 8]), Res()) for i in range(2)]
    decallR = [(sb("decall%d" % i, [128, NH, 16]), Res()) for i in range(2)]
    ewlb, R_ewlb = ewlbR[0]
    decall, R_decall = decallR[0]
    small = sb("small", [128, 64]); R_small = Res()
    R_smk = [Res(), Res()]; R_smq = [Res(), Res()]; R_smh = [Res() for _ in range(4)]; R_sma = Res()
    ssq = sb("ssq", [128, 8]); R_ssq = [Res() for _ in range(4)]
    junk = sb("junk", [128, 256], BF16); R_junk = Res()
    kfR = [(sb("kf%d" % i, [128, 512]), Res()) for i in range(2)]
    knR = [(sb("kn%d" % i, [128, 256]), Res()) for i in range(2)]
    KtmR = [(sb("Ktm%d" % i, [128, 256], BF16), Res()) for i in range(2)]
    kT = sb("kT", [128, 2, 640], BF16); R_kT = [Res() for _ in range(5)]
    Vex = sb("Vex", [128, 5, 4, 65], BF16); R_Vex = [Res() for _ in range(5)]
    qfR = [(sb("qf%d" % i, [128, 512]), Res()) for i in range(2)]
    PT = [(sb("PT%d" % i, [128, 2, 512], BF16), Res()) for i in range(2)]
    gcols = sb("gcols", [128, 48]); bif = sb("bif", [4, 2]); bif15 = sb("bif15", [4, 2])
    gk_b = sb("gk_b", [128, 64]); gq_b = sb("gq_b", [128, 64]); sink_b = sb("sink_b", [128, 16])
    ident_b = sb("ident_b", [128, 128], BF16); ident_f = sb("ident_f", [128, 128])
    mlmask = sb("mlmask", [128, 64], BF16); smask = sb("smask", [64, 64], BF16)
    amask = sb("amask", [128, 256], BF16); cmask = sb("cmask", [128, 4], BF16)
    rowmask = sb("rowmask", [64, 16]); eye4 = sb("eye4", [4, 4]); ones4 = sb("ones4", [4, 128])
    flag = sb("flag", [128, 1])
    R_const = Res("const")
    s_m0 = sb("s_m0", [4, NS]); R_sm0 = Res()
    s_mo = sb("s_mo", [4, NS]); R_smo = Res()
    Cin = [(None, None, S.newsem("ci%d" % i)) for i in range(2)]
    fdummy = sb("fdummy", [128, 2]); R_fd = Res()
    nout = sb("nout", [128, NS * NH]); R_nout = Res()
    noutT = sb("noutT", [NS * NH, 128]); R_noutT = Res()
    samp_sems = []
    R_cvb = Res()
    KcT = sb("KcT", [128, NS, 128], BF16); R_KcT = Res()
    PTc = sb("PTc", [128, NS, 4, ST], BF16); R_PTc = [Res() for _ in range(NS)]
    PTn = sb("PTn", [64, 4, ST], BF16); R_PTn = Res()

    ps_all = es.enter_context(nc.psum_tensor("ps_all", [128, 8, 512], F32))
    R_ps = [Res("ps%d" % i) for i in range(8)]
    ps_ctr = [0]

    ps_reserved = set()

    def psum(reserve=False):
        while True:
            i = ps_ctr[0] % 8
            ps_ctr[0] += 1
            if i not in ps_reserved:
                break
        if reserve:
            ps_reserved.add(i)
        return ps_all[:, i, :], R_ps[i]

    def psum_release(pr):
        ps_reserved.discard(R_ps.index(pr))

    tm_ctr = [0]
    fm_ctr = [0]

    def next_tm():
        i = tm_ctr[0] % len(TM); tm_ctr[0] += 1
        return TM[i]

    def next_fm():
        i = fm_ctr[0] % len(FM); fm_ctr[0] += 1
        return FM[i]

    s_ld = S.newsem("ld")
    s_st = S.newsem("st")
    s_ldx2 = [[S.newsem("ldx%d_%d" % (j, i)) for i in range(4)] for j in range(2)]
    s_sty2 = [[S.newsem("sty%d_%d" % (j, i)) for i in range(4)] for j in range(2)]
    s_ldx = s_ldx2[0]
    s_sty = s_sty2[0]
    s_ck = S.newsem("ck")
    s_cv = S.newsem("cv")
    s_co = [S.newsem("co%d" % i) for i in range(2)]
    s_c = S.newsem("cst")

    cfns = []
    for dst, src in ((gcols, gcols_d), (bif, bif_d), (gk_b, gk_d), (gq_b, gq_d), (sink_b, sink_d),
                     (ident_b, identb_d), (ident_f, identf_d), (mlmask, mlmask_d), (smask, smask_d),
                     (amask, amask_d), (cmask, cmask_d), (rowmask, rowmask_d), (eye4, eye4_d),
                     (ones4, ones4_d), (flag, flag_d), (s_m0, smT_d)):
        cfns.append(I("dma_start", out=dst[:], in_=src[:]))
    S.dma_group("sp", cfns, writes=[R_const], dsem=s_c)
    S.op("dve", I("tensor_scalar", out=bif15[:], in0=bif[:], scalar1=1.0 / 15.0, scalar2=None, op0=ALU.mult),
         reads=[R_const], writes=[R_const])
    S.op("act", I("activation", out=sink_b[:], in_=sink_b[:], func=AF.Exp), reads=[R_const], writes=[R_const])
    S.op("dve", I("memset", Cst[:], 0.0), writes=[R_Cst])
    S.op("dve", I("memset", carryA[:], 0.0), writes=[R_cA])
    S.op("dve", I("memset", ddg[:], 0.0), writes=[R_ddg])
    S.op("dve", I("memset", carryM[:], 0.0), writes=[R_cM])
    S.op("dve", I("memset", Vx[:], 0.0), writes=R_Vx)
    S.op("dve", I("memset", Vex[:], 1.0), writes=R_Vex)
    S.op("dve", I("memset", kT[:], 0.0), writes=R_kT)
    S.op("dve", I("memset", PTc[:], 0.0), writes=R_PTc)

    wq = []
    conv = {}
    conv_sem = {}
    wstate = {"issued": 0, "used": 0}

    def wview(slot_t, kc, cols):
        return slot_t[:, 0:kc * cols].rearrange("p (k c) -> p k c", c=cols)

    def chunk_dmas(kind, layer, mode):
        sc = (mode == "sc")

        def rows(w, k, c0, c1):
            return w[k * 128:(k + 1) * 128, c0:c1]
        Win = sc_in if sc else w_in_d
        out = []
        if kind == "inB":
            for k in range(8):
                if mode == "part":
                    out.append((k, 1024, 1032, rows(Win, k, 3072, 3080)))
                else:
                    out.append((k, 0, 1032, rows(Win, k, 2048, 3080)))
            return 8, 1032, out, ["in"]
        if kind == "inQK":
            for k in range(8):
                if mode == "part":
                    out.append((k, 512, 1024, rows(Win, k, 512, 1024)))
                else:
                    out.append((k, 0, 1024, rows(Win, k, 0, 1024)))
            return 8, 1024, out, ["in"]
        if kind == "inV":
            for k in range(8):
                out.append((k, 0, 1024, rows(Win, k, 1024, 2048)))
            return 8, 1024, out, ["in"]
        if kind == "out":
            for k in range(8):
                out.append((k, 0, 1024, rows(sc_out if sc else w_out_d, k, 0, 1024)))
            return 8, 1024, out, ["out"]
        if kind.startswith("w1_"):
            c0 = int(kind[3:]) * 1024
            W = sc_w1 if sc else w1_d
            for k in range(8):
                out.append((k, 0, 1024, W[layer, k * 128:(k + 1) * 128, c0:c0 + 1024]))
            return 8, 1024, out, ["w1_%d" % layer]
        if kind.startswith("w2_"):
            r0 = int(kind[3:]) * 1024
            W = sc_w2 if sc else w2_d
            for k4 in range(2):
                src = W[layer, r0 + k4 * 512:r0 + (k4 + 1) * 512, :].rearrange("(k p) c -> p k c", p=128)
                out.append(((k4 * 4, k4 * 4 + 4), 0, 1024, src))
            return 8, 1024, out, ["w2_%d" % layer]
        if kind == "kv":
            for k in range(8):
                out.append((k, 0, 512, rows(sc_kv if sc else w_kv_d, k, 0, 512)))
            return 8, 512, out, ["kv"]
        if kind == "q":
            for k in range(8):
                out.append((k, 0, 1024, rows(sc_q if sc else w_q_d, k, 0, 1024)))
            return 8, 1024, out, ["q"]
        if kind == "o":
            for k in range(8):
                out.append((k, 0, 1024, rows(sc_o if sc else w_o_d, k, 0, 1024)))
            return 8, 1024, out, ["o"]
        raise ValueError(kind)

    def plan_chunk(kind, layer=0, mode="sc"):
        wq.append((kind, layer, mode))

    def issue_load(idx):
        kind, layer, mode = wq[idx]
        slot_t, slot_r, slot_s = wslot[idx % NSLOT]
        kc, cols, dmas, deps = chunk_dmas(kind, layer, mode)
        v = wview(slot_t, kc, cols)

        def dst_of(k, c0, c1):
            return v[:, k[0]:k[1], c0:c1] if isinstance(k, tuple) else v[:, k, c0:c1]
        fns = [I("dma_start", out=dst_of(k, c0, c1), in_=src, max_dma_last_dim=8192) for (k, c0, c1, src) in dmas]
        if mode == "sc":
            S.dma_group("sp", fns, reads=[conv[d] for d in deps], writes=[slot_r], dsem=slot_s)
        else:
            S.dma_group("pool", fns, writes=[slot_r], dsem=slot_s)
        if mode == "f32wb":
            _a, _b, sdmas, _c = chunk_dmas(kind, layer, "sc")
            wfns = [I("dma_start", out=ssrc, in_=dst_of(k, c0, c1)) for (k, c0, c1, ssrc) in sdmas]
            name = deps[0]
            if name not in conv:
                conv[name] = Res("sc_" + name)
                conv_sem[name] = S.newsem("wb_" + name)
            S.dma_group("sp", wfns, reads=[slot_r], writes=[conv[name]], dsem=conv_sem[name])

    def wget(kind):
        idx = wstate["used"]
        assert wq[idx][0] == kind, (wq[idx], kind)
        while wstate["issued"] < min(len(wq), idx + NSLOT):
            issue_load(wstate["issued"]); wstate["issued"] += 1
        slot_t, slot_r, _ = wslot[idx % NSLOT]
        kc, cols, _d, _e = chunk_dmas(*wq[idx])
        return wview(slot_t, kc, cols), slot_r

    def wdone():
        wstate["used"] += 1

    def pipeline(N, phases, lag=1):
        for step in range(N + lag * (len(phases) - 1)):
            for p, ph in enumerate(phases):
                u = step - p * lag
                if 0 <= u < N:
                    ph(u)

    def evac(i, out, in_, reads, writes, scale=None):
        if i % 2 == 0:
            if scale is None:
                S.op("act", I("activation", out=out, in_=in_, func=AF.Copy), reads=reads, writes=writes)
            else:
                S.op("act", I("activation", out=out, in_=in_, func=AF.Copy, scale=scale), reads=reads, writes=writes)
        else:
            if scale is None:
                S.op("dve", I("tensor_copy", out=out, in_=in_), reads=reads, writes=writes)
            else:
                S.op("dve", I("tensor_scalar", out=out, in0=in_, scalar1=scale, scalar2=None, op0=ALU.mult),
                     reads=reads, writes=writes)

    def rsqrt_small(dst, src, n, scale, reads_writes):
        S.op("act", I("activation", out=dst, in_=src, func=AF.Ln, scale=scale, bias=eps_c[0:dst.shape[0], :]),
             reads=reads_writes + [R_const], writes=reads_writes)
        S.op("act", I("activation", out=dst, in_=dst, func=AF.Exp, scale=-0.5),
             reads=reads_writes, writes=reads_writes)

    eps_c = sb("eps_c", [128, 1])
    S.op("dve", I("memset", eps_c[:], EPS), writes=[R_const])

    def norm_T(NB, PTK, gidx):
        tm, tmr = next_tm()
        fm, fmr = next_fm()
        P = slice(0, PTK)
        for b in range(NB):
            S.op("act", I("activation", out=tm[P, b, :], in_=xres[P, b, :], func=AF.Square, accum_out=ssq[P, b:b + 1]),
                 reads=[R_x[b]], writes=[tmr[b], R_ssq[b]])
            rsqrt_small(ssq[P, b:b + 1], ssq[P, b:b + 1], 1, 1.0 / D, [R_ssq[b]])
            S.op("dve", I("tensor_scalar", out=tm[P, b, :], in0=xres[P, b, :], scalar1=ssq[P, b:b + 1], scalar2=None, op0=ALU.mult),
                 reads=[R_x[b], R_ssq[b]], writes=[tmr[b]])
            tm_to_fm_blk(tm, tmr, fm, fmr, b, PTK, gidx)
        return fm, fmr

    def norm_T2(NB, PTK, g1, g2):
        tm, tmr = next_tm()
        fm1, fmr1 = next_fm()
        fm2, fmr2 = next_fm()
        P = slice(0, PTK)
        for b in range(NB):
            S.op("act", I("activation", out=tm[P, b, :], in_=xres[P, b, :], func=AF.Square, accum_out=ssq[P, b:b + 1]),
                 reads=[R_x[b]], writes=[tmr[b], R_ssq[b]])
            rsqrt_small(ssq[P, b:b + 1], ssq[P, b:b + 1], 1, 1.0 / D, [R_ssq[b]])
            S.op("dve", I("tensor_scalar", out=tm[P, b, :], in0=xres[P, b, :], scalar1=ssq[P, b:b + 1], scalar2=None, op0=ALU.mult),
                 reads=[R_x[b], R_ssq[b]], writes=[tmr[b]])
            pt, pr = psum()
            ptb = pt.bitcast(BF16)
            fns = [I("transpose", out=ptb[:, k * PTK:(k + 1) * PTK], in_=tm[0:PTK, b, k * 128:(k + 1) * 128], identity=ident_b[0:PTK, 0:PTK])
                   for k in range(8)]
            S.group("pe", fns, reads=[tmr[b], R_const], writes=[pr])
            src = ptb[:, 0:8 * PTK].rearrange("p (k t) -> p k t", t=PTK)
            for (fm, fmr, g) in ((fm1, fmr1, g1), (fm2, fmr2, g2)):
                S.op("dve", I("tensor_tensor", out=fm[:, :, b * PTK:(b + 1) * PTK], in0=src,
                              in1=gcols[:, g * 8:(g + 1) * 8].unsqueeze(2).to_broadcast([128, 8, PTK]), op=ALU.mult),
                     reads=[pr, R_const], writes=[fmr[b]])
        return fm1, fmr1, fm2, fmr2

    def tm_to_fm_blk(tm, tmr, fm, fmr, b, PTK, gidx, use_act=False):
        pt, pr = psum()
        ptb = pt.bitcast(BF16)
        fns = [I("transpose", out=ptb[:, k * PTK:(k + 1) * PTK], in_=tm[0:PTK, b, k * 128:(k + 1) * 128], identity=ident_b[0:PTK, 0:PTK])
               for k in range(8)]
        S.group("pe", fns, reads=[tmr[b], R_const], writes=[pr])
        dst = fm[:, :, b * PTK:(b + 1) * PTK]
        src = ptb[:, 0:8 * PTK].rearrange("p (k t) -> p k t", t=PTK)
        if gidx is not None:
            S.op("dve", I("tensor_tensor", out=dst, in0=src, in1=gcols[:, gidx * 8:(gidx + 1) * 8].unsqueeze(2).to_broadcast([128, 8, PTK]),
                          op=ALU.mult), reads=[pr, R_const], writes=[fmr[b]])
        elif use_act:
            S.op("act", I("activation", out=dst, in_=src, func=AF.Copy), reads=[pr], writes=[fmr[b]])
        else:
            S.op("dve", I("tensor_copy", out=dst, in_=src), reads=[pr], writes=[fmr[b]])

    def tm_to_fm(tm, tmr, fm, fmr, NB, PTK, gidx):
        for b in range(NB):
            tm_to_fm_blk(tm, tmr, fm, fmr, b, PTK, gidx, use_act=(b % 2 == 0))

    def proj_tm(fm, fmr, NB, PTK, wv, wr, c0, ncols, consume):
        for b in range(NB):
            pt, pr = psum()
            fns = [I("matmul", pt[0:PTK, 0:ncols], lhsT=fm[:, k, b * PTK:(b + 1) * PTK],
                                                 rhs=wv[:, k, c0:c0 + ncols], start=(k == 0), stop=(k == 7))
                   for k in range(8)]
            S.group("pe", fns, reads=[fmr[b], wr], writes=[pr])
            consume(b, pt, pr)

    def resid_add(NB, PTK, fm, fmr, wv, wr, nk):
        for b in range(NB):
            for half in range(2):
                pt, pr = psum()
                fns = [I("matmul", pt[0:PTK, :], lhsT=fm[:, k, b * PTK:(b + 1) * PTK], rhs=wv[:, k, half * 512:(half + 1) * 512],
                         start=(k == 0), stop=(k == nk - 1)) for k in range(nk)]
                S.group("pe", fns, reads=[fmr[b], wr], writes=[pr])
                S.op("dve", I("tensor_tensor", out=xres[0:PTK, b, half * 512:(half + 1) * 512], in0=pt[0:PTK, :],
                              in1=xres[0:PTK, b, half * 512:(half + 1) * 512], op=ALU.add),
                     reads=[pr, R_x[b]], writes=[R_x[b]])

    MLP_ORDER = ("w1_0", "w1_1", "w2_0", "w1_2", "w2_1", "w1_3", "w2_2", "w2_3")

    def mlp(NB, PTK, layer, hook=None):
        T = NB * PTK
        S.tag = S.tile + ':mlp%d' % layer
        fm, fmr = norm_T(NB, PTK, 1 if layer == 0 else 4)

        def w1_part(part):
            hb = (part % 2) * 8
            wv, wr = wget("w1_%d" % part)
            for f in range(8):
                pt, pr = psum()
                fns = [I("matmul", pt[:, 0:T], lhsT=wv[:, k, f * 128:(f + 1) * 128], rhs=fm[:, k, 0:T],
                         start=(k == 0), stop=(k == 7)) for k in range(8)]
                S.group("pe", fns, reads=fmr[0:NB] + [wr], writes=[pr])
                S.op("act", I("activation", out=hT[:, hb + f, 0:T], in_=pt[:, 0:T], func=AF.Relu), reads=[pr], writes=[R_hT[hb + f]])
                S.op("dve", I("tensor_tensor", out=hT[:, hb + f, 0:T], in0=hT[:, hb + f, 0:T], in1=hT[:, hb + f, 0:T], op=ALU.mult),
                     reads=[R_hT[hb + f]], writes=[R_hT[hb + f]])
            wdone()

        def w2_part(part):
            hb = (part % 2) * 8
            wv, wr = wget("w2_%d" % part)
            for b in range(NB):
                for half in range(2):
                    pt, pr = psum()
                    fns = [I("matmul", pt[0:PTK, :], lhsT=hT[:, hb + k, b * PTK:(b + 1) * PTK], rhs=wv[:, k, half * 512:(half + 1) * 512],
                             start=(k == 0), stop=(k == 7)) for k in range(8)]
                    S.group("pe", fns, reads=R_hT[hb:hb + 8] + [wr], writes=[pr])
                    S.op("dve", I("tensor_tensor", out=xres[0:PTK, b, half * 512:(half + 1) * 512], in0=pt[0:PTK, :],
                                  in1=xres[0:PTK, b, half * 512:(half + 1) * 512], op=ALU.add),
                         reads=[pr, R_x[b]], writes=[R_x[b]])
            wdone()
        for ii, nm in enumerate(MLP_ORDER):
            if hook is not None and ii == len(MLP_ORDER) - 2:
                hook()
            (w1_part if nm.startswith("w1") else w2_part)(int(nm[3:]))

    def gates(fm, fmr, wv, wr, T, sample):
        g = {n: grow[n][0] for n in grow}
        gr = {n: grow[n][1] for n in grow}
        for nm, col, bcol in (("ti", 1024, 0), ("tf", 1028, 1)):
            pt, pr = psum()
            fns = [I("matmul", pt[0:4, 0:T], lhsT=wv[:, k, col:col + 4], rhs=fm[:, k, 0:T],
                                                      start=(k == 0), stop=(k == 7)) for k in range(8)]
            S.group("pe", fns, reads=fmr + [wr], writes=[pr])
            yield
            S.op("act", I("activation", out=g[nm][:, 0:T], in_=pt[0:4, 0:T], func=AF.Tanh,
                                                                      scale=1.0 / 15.0, bias=bif15[:, bcol:bcol + 1]),
                 reads=[pr, R_const], writes=[gr[nm]])
            yield
        S.op("act", I("activation", out=g["t1"][:, 0:T], in_=g["tf"][:, 0:T], func=AF.Exp, scale=-15.0),
             reads=[gr["tf"]], writes=[gr["t1"]])
        yield
        S.op("act", I("activation", out=g["t1"][:, 0:T], in_=g["t1"][:, 0:T], func=AF.Ln, bias=1.0),
             reads=[gr["t1"]], writes=[gr["t1"]])
        yield
        cm, Mq, Mp, dec = (gsm[n][0] for n in ("cm", "Mq", "Mp", "dec"))
        r_cm, r_Mq, r_Mp, r_dec = (gsm[n][1] for n in ("cm", "Mq", "Mp", "dec"))
        if not sample:
            NC = 1
            CL = T
            S.op("dve", I("tensor_tensor_scan", out=g["A"][:, 0:T], data0=g["t1"][:, 0:T], data1=g["t1"][:, 0:T],
                                                       initial=carryA[:, 0:1], op0=ALU.add, op1=ALU.max),
                 reads=[gr["t1"], R_cA], writes=[gr["A"]])
            yield
        else:
            NC = NS
            CL = 4
            A3 = g["A"][:, 0:T].rearrange("p (c j) -> p c j", j=4)
            l3 = g["t1"][:, 0:T].rearrange("p (c j) -> p c j", j=4)
            S.op("dve", I("tensor_copy", out=A3[:, :, 0:1], in_=l3[:, :, 0:1]), reads=[gr["t1"]], writes=[gr["A"]])
            yield
            for j in range(1, 4):
                S.op("dve", I("tensor_tensor", out=A3[:, :, j:j + 1], in0=A3[:, :, j - 1:j], in1=l3[:, :, j:j + 1],
                                                                 op=ALU.add),
                     reads=[gr["t1"], gr["A"]], writes=[gr["A"]])
                yield
        S.op("dve", I("scalar_tensor_tensor", out=g["gp"][:, 0:T], in0=g["ti"][:, 0:T], scalar=15.0, in1=g["A"][:, 0:T],
                                                     op0=ALU.mult, op1=ALU.add),
             reads=[gr["ti"], gr["A"]], writes=[gr["gp"]])
        yield
        gp3 = g["gp"][:, 0:T].rearrange("p (c j) -> p c j", j=CL)
        A3 = g["A"][:, 0:T].rearrange("p (c j) -> p c j", j=CL)
        S.op("dve", I("tensor_reduce", out=cm[:, 0:NC], in_=gp3, axis=AX.X, op=ALU.max),
             reads=[gr["gp"]], writes=[r_cm])
        yield
        if not sample:
            S.op("dve", I("tensor_tensor_scan", out=Mq[:, 0:NC], data0=cm[:, 0:NC], data1=cm[:, 0:NC],
                                                       initial=carryM[:, 0:1], op0=ALU.max, op1=ALU.max),
                 reads=[r_cm, R_cM], writes=[r_Mq])
            yield
            S.op("dve", I("tensor_copy", out=Mp[:, 0:1], in_=carryM[:, 0:1]), reads=[R_cM], writes=[r_Mp])
            yield
            if NC > 1:
                S.op("dve", I("tensor_copy", out=Mp[:, 1:NC], in_=Mq[:, 0:NC - 1]), reads=[r_Mq, r_Mp], writes=[r_Mp])
                yield
        else:
            S.op("dve", I("tensor_tensor", out=Mq[:, 0:NC], in0=cm[:, 0:NC], in1=s_m0[:, 0:NC], op=ALU.max),
                 reads=[r_cm, R_const], writes=[r_Mq])
            yield
            S.op("dve", I("tensor_copy", out=Mp[:, 0:NC], in_=s_m0[:, 0:NC]), reads=[R_const], writes=[r_Mp])
            yield
        S.op("dve", I("tensor_tensor", out=dec[:, 0:NC], in0=Mp[:, 0:NC], in1=Mq[:, 0:NC], op=ALU.subtract),
             reads=[r_Mp, r_Mq], writes=[r_dec])
        yield
        S.op("act", I("activation", out=dec[:, 0:NC], in_=dec[:, 0:NC], func=AF.Exp), reads=[r_dec], writes=[r_dec])
        yield
        Mqb = Mq[:, 0:NC].unsqueeze(2).to_broadcast([4, NC, CL])
        t1_3 = g["t1"][:, 0:T].rearrange("p (c j) -> p c j", j=CL)
        t2_3 = g["t2"][:, 0:T].rearrange("p (c j) -> p c j", j=CL)
        S.op("dve", I("tensor_tensor", out=t1_3, in0=gp3, in1=Mqb, op=ALU.subtract),
             reads=[gr["gp"], r_Mq, gr["t1"]], writes=[gr["t1"]])
        yield
        S.op("act", I("activation", out=g["t1"][:, 0:T], in_=g["t1"][:, 0:T], func=AF.Exp), reads=[gr["t1"]], writes=[gr["t1"]])
        yield
        S.op("dve", I("tensor_tensor", out=t2_3, in0=A3, in1=Mqb, op=ALU.subtract),
             reads=[gr["A"], r_Mq], writes=[gr["t2"]])
        yield
        S.op("act", I("activation", out=g["t2"][:, 0:T], in_=g["t2"][:, 0:T], func=AF.Exp), reads=[gr["t2"]], writes=[gr["t2"]])
        yield
        PTK = min(T, 128)
        NB = T // PTK
        pt, pr = psum()
        fns = []
        for b in range(NB):
            for j, nm in enumerate(("t1", "t2")):
                fns.append(I("transpose", out=pt[0:PTK, b * 8 + j * 4:b * 8 + j * 4 + 4],
                                                                   in_=g[nm][:, b * PTK:(b + 1) * PTK],
                                                                   identity=ident_f[0:4, 0:4]))
        S.group("pe", fns, reads=[gr["t1"], gr["t2"], R_const], writes=[pr])
        yield
        S.op("dve", I("tensor_copy", out=ewlb[0:PTK, 0:NB, :], in_=pt[0:PTK, 0:NB * 8].rearrange("p (b j) -> p b j", j=8)),
             reads=[pr], writes=[R_ewlb])
        yield
        S.op("dve", I("tensor_tensor", out=ddg[:, :, 0:NC], in0=dec[:, 0:NC].unsqueeze(1).to_broadcast([4, NH, NC]),
                                              in1=eye4[:, :].unsqueeze(2).to_broadcast([4, NH, NC]), op=ALU.mult),
             reads=[r_dec, R_const], writes=[R_ddg])
        yield
        pt2, pr2 = psum()
        S.group("pe", [I("matmul", pt2[:, 0:NH * 16], lhsT=ones4[:, :], rhs=ddg[:, :, :].rearrange("p h c -> p (h c)"),
                                          start=True, stop=True)],
                reads=[R_ddg, R_const], writes=[pr2])
        yield
        S.op("dve", I("tensor_copy", out=decall[:, :, :], in_=pt2[:, 0:NH * 16].rearrange("p (h c) -> p h c", c=16)),
             reads=[pr2], writes=[R_decall])
        yield
        if not sample:
            S.op("dve", I("tensor_copy", out=carryA[:, 0:1], in_=g["A"][:, T - 1:T]), reads=[gr["A"]], writes=[R_cA])
            yield
            S.op("dve", I("tensor_copy", out=carryM[:, 0:1], in_=Mq[:, NC - 1:NC]), reads=[r_Mq], writes=[R_cM])
            yield
        else:
            S.op("dve", I("tensor_tensor", out=s_mo[:, 0:NC], in0=Mq[:, 0:NC], in1=A3[:, :, 3:4].rearrange("p c j -> p (c j)"),
                                                  op=ALU.subtract),
                 reads=[r_Mq, gr["A"]], writes=[R_smo])
            yield

    def mlstm(NB, PTK, do_out, sample, pre=None):
        T = NB * PTK
        S.tag = S.tile + ':ml_proj'
        fm, fmr = pre if pre is not None else norm_T(NB, PTK, 0)
        wv, wr = wget("inB")
        gg = gates(fm, fmr, wv, wr, T, sample)

        def gstep(n=1):
            for _ in range(n):
                next(gg, None)
        gstep(4)
        if do_out:
            otm, otr = next_tm()
            for half in range(2):
                def cons(b, pt, pr, half=half):
                    S.op("act", I("activation", out=otm[0:PTK, b, half * 512:(half + 1) * 512], in_=pt[0:PTK, :],
                                                       func=AF.Sigmoid), reads=[pr], writes=[otr[b]])
                    gstep(2)
                proj_tm(fm, fmr, NB, PTK, wv, wr, half * 512, 512, cons)
        wdone()
        wv, wr = wget("inQK")
        if do_out:
            qk, qkr = next_fm()
            for c in range(8):
                pt, pr = psum()
                fns = [I("matmul", pt[:, 0:T], lhsT=wv[:, k, c * 128:(c + 1) * 128], rhs=fm[:, k, 0:T],
                                                     start=(k == 0), stop=(k == 7)) for k in range(8)]
                S.group("pe", fns, reads=fmr + [wr], writes=[pr])
                evac(c, qk[:, c, 0:T], pt[:, 0:T], [pr], qkr, scale=(DK ** -0.5 if c >= 4 else None))
                gstep(2)

        def cons_k(b, pt, pr):
            evac(b, k_tm[0:PTK, b, :], pt[0:PTK, :], [pr], [R_ktm[b]], scale=DK ** -0.5)
            gstep(3)
        proj_tm(fm, fmr, NB, PTK, wv, wr, 512, 512, cons_k)
        wdone()
        for _ in gg:
            pass
        wv, wr = wget("inV")
        for half in range(2):
            def cons_v(b, pt, pr, half=half):
                for j in range(2):
                    h = half * 2 + j
                    evac(j, Vx[0:PTK, b, h, 0:DV], pt[0:PTK, j * 256:(j + 1) * 256], [pr, R_ewlb], [R_Vx[b]],
                         scale=ewlb[0:PTK, b, h:h + 1])
            proj_tm(fm, fmr, NB, PTK, wv, wr, half * 512, 512, cons_v)
        for b in range(NB):
            S.op("dve", I("tensor_copy", out=Vx[0:PTK, b, :, DV:DVX], in_=ewlb[0:PTK, b, 0:4].unsqueeze(2)),
                 reads=[R_ewlb], writes=[R_Vx[b]])
        wdone()
        S.tag = S.tile + ':ml_core'
        if sample:
            mlstm_core_sample(qk, qkr, otm, otr)
        else:
            mlstm_core(NB, do_out, qk if do_out else None, qkr if do_out else None,
                       otm if do_out else None, otr if do_out else None)
        if do_out:
            S.tag = S.tile + ':ml_out'
            fo, fo_r = next_fm()
            tm_to_fm(otm, otr, fo, fo_r, NB, PTK, 5)
            wv, wr = wget("out")
            resid_add(NB, PTK, fo, fo_r, wv, wr, 8)
            wdone()

    def selx(i):
        nonlocal xres, R_x, s_ldx, s_sty
        xres, R_x = xbufs[i]
        s_ldx = s_ldx2[i]
        s_sty = s_sty2[i]

    def sel(par):
        nonlocal ewlb, R_ewlb, decall, R_decall, k_tm, R_ktm
        ewlb, R_ewlb = ewlbR[par]
        decall, R_decall = decallR[par]
        k_tm, R_ktm = ktmR[par]

    pre_fm = {}

    def prefix_A(k, r0, NB):
        sel(k % 2)
        S.tile = 'pre%d' % r0
        S.tag = S.tile + ':ml_proj'
        load_x(xpre[r0:r0 + NB * 128, :], NB, 128)
        T = NB * 128
        fm, fmr = norm_T(NB, 128, 0)
        wv, wr = wget("inB")
        gg = gates(fm, fmr, wv, wr, T, False)
        for _ in range(4):
            next(gg, None)
        wdone()
        wv, wr = wget("inQK")

        def cons_k(b, pt, pr):
            evac(b, k_tm[:, b, :], pt[:, :], [pr], [R_ktm[b]], scale=DK ** -0.5)
            for _ in range(4):
                next(gg, None)
        proj_tm(fm, fmr, NB, 128, wv, wr, 512, 512, cons_k)
        wdone()
        for _ in gg:
            pass
        pre_fm[k] = (fm, fmr)

    def prefix_B(k, r0, NB):
        sel(k % 2)
        S.tile = 'pre%d' % r0
        S.tag = S.tile + ':ml_core'
        fm, fmr = pre_fm.pop(k)
        wv, wr = wget("inV")
        for half in range(2):
            def cons_v(b, pt, pr, half=half):
                for j in range(2):
                    h = half * 2 + j
                    evac(j, Vx[:, b, h, 0:DV], pt[:, j * 256:(j + 1) * 256], [pr, R_ewlb], [R_Vx[b]],
                         scale=ewlb[:, b, h:h + 1])
            proj_tm(fm, fmr, NB, 128, wv, wr, half * 512, 512, cons_v)
        for b in range(NB):
            S.op("dve", I("tensor_copy", out=Vx[:, b, :, DV:DVX], in_=ewlb[:, b, 0:4].unsqueeze(2)),
                 reads=[R_ewlb], writes=[R_Vx[b]])
        wdone()
        mlstm_core(NB, False, None, None, None, None)

    def head_out(pn, pnr, b, h, PTK, otm, otr):
        P = slice(0, PTK)
        sm = small
        c0 = h * 8
        S.op("dve", I("tensor_scalar", out=sm[P, c0:c0 + 1], in0=pn[P, DV:DVX], scalar1=-1.0, scalar2=None, op0=ALU.mult),
             reads=[pnr], writes=[R_smh[h]])
        S.op("dve", I("tensor_tensor", out=sm[P, c0:c0 + 1], in0=sm[P, c0:c0 + 1], in1=pn[P, DV:DVX], op=ALU.max),
             reads=[pnr, R_smh[h]], writes=[R_smh[h]])
        S.op("dve", I("tensor_tensor", out=sm[P, c0:c0 + 1], in0=sm[P, c0:c0 + 1], in1=ewlb[P, b, 4 + h:5 + h], op=ALU.max),
             reads=[R_ewlb, R_smh[h]], writes=[R_smh[h]])
        S.op("dve", I("reciprocal", out=sm[P, c0 + 1:c0 + 2], in_=sm[P, c0:c0 + 1]), reads=[R_smh[h]], writes=[R_smh[h]])
        S.op("act", I("activation", out=junk[P, 0:DV], in_=pn[P, 0:DV], func=AF.Square, scale=sm[P, c0 + 1:c0 + 2],
                                           accum_out=sm[P, c0 + 2:c0 + 3]),
             reads=[pnr, R_smh[h]], writes=[R_junk, R_smh[h]])
        rsqrt_small(sm[P, c0 + 3:c0 + 4], sm[P, c0 + 2:c0 + 3], 1, 1.0 / DV, [R_smh[h]])
        S.op("dve", I("tensor_tensor", out=sm[P, c0 + 4:c0 + 5], in0=sm[P, c0 + 3:c0 + 4], in1=sm[P, c0 + 1:c0 + 2], op=ALU.mult),
             reads=[R_smh[h]], writes=[R_smh[h]])
        S.op("dve", I("scalar_tensor_tensor", out=otm[P, b, h * DV:(h + 1) * DV], in0=pn[P, 0:DV], scalar=sm[P, c0 + 4:c0 + 5],
                                                     in1=otm[P, b, h * DV:(h + 1) * DV], op0=ALU.mult, op1=ALU.mult),
             reads=[pnr, R_smh[h], otr[b]], writes=[otr[b]])

    def mlstm_core(NB, do_out, qk, qkr, otm, otr):
        T = NB * 128
        S.op("dve", I("tensor_tensor", out=Cst[:, :, :], in0=Cst[:, :, :], in1=decall[:, :, 0:1].to_broadcast([128, NH, DVX]),
                      op=ALU.mult), reads=[R_decall, R_Cst], writes=[R_Cst])
        if do_out:
            S.op("act", I("activation", out=Cb[:, :, :], in_=Cst[:, :, :], func=AF.Copy), reads=[R_Cst], writes=[R_Cb])
            offs = [sum((NB - ii) * 128 for ii in range(i)) for i in range(NB)]
            def phA(h):
                sw, swr = SwTb[h % 2]
                for i in range(NB):
                    n = (NB - i) * 128
                    pt, pr = psum()
                    S.group("pe", [I("matmul", pt[:, 0:n], lhsT=qk[:, 4 + h, i * 128:(i + 1) * 128], rhs=qk[:, h, i * 128:T],
                                     start=True, stop=True)], reads=qkr, writes=[pr])
                    S.op("dve", I("tensor_tensor", out=sw[:, offs[i]:offs[i] + 128], in0=pt[:, 0:128], in1=amask[:, 128:256], op=ALU.mult),
                         reads=[pr, R_const], writes=[swr])
                    if n > 128:
                        S.op("act", I("activation", out=sw[:, offs[i] + 128:offs[i] + n], in_=pt[:, 128:n], func=AF.Copy),
                             reads=[pr], writes=[swr])

            def phB(h):
                sw, swr = SwTb[h % 2]
                for j in range(NB):
                    pn, pnr = psum()
                    fns = [I("matmul", pn[:, 0:DVX], lhsT=sw[:, offs[i] + (j - i) * 128:offs[i] + (j - i + 1) * 128], rhs=Vx[:, i, h, :],
                             start=(i == 0), stop=False) for i in range(j + 1)]
                    fns.append(I("matmul", pn[:, 0:DVX], lhsT=qk[:, h, j * 128:(j + 1) * 128], rhs=Cb[:, h, :], start=False, stop=True))
                    S.group("pe", fns, reads=[swr, R_Cb] + qkr + [R_Vx[i] for i in range(j + 1)], writes=[pnr])
                    head_out(pn, pnr, j, h, 128, otm, otr)
            pipeline(NH, [phA, phB], lag=1)
        for h in range(NH):
            pc, pcr = psum()
            fns = [I("matmul", pc[:, 0:DVX], lhsT=k_tm[:, i, h * DK:(h + 1) * DK], rhs=Vx[:, i, h, :], start=(i == 0), stop=(i == NB - 1))
                   for i in range(NB)]
            S.group("pe", fns, reads=[R_ktm[i] for i in range(NB)] + [R_Vx[i] for i in range(NB)], writes=[pcr])
            S.op("dve", I("tensor_tensor", out=Cst[:, h, :], in0=pc[:, 0:DVX], in1=Cst[:, h, :], op=ALU.add),
                 reads=[pcr, R_Cst], writes=[R_Cst])

    def fence(old, new):
        S.op("dve", I("memset", fdummy[:, :], 0.0), writes=list(old) + list(new) + [R_fd])

    def mlstm_core_sample(qk, qkr, otm, otr):
        P = slice(0, ST)
        arena = hT[:, :, :].rearrange("p a b -> p (a b)")
        off = [0]

        def carve(n, parts=128):
            v = arena[0:parts, off[0]:off[0] + n]
            off[0] += n + (n % 2)
            return v
        CinR = [(carve(516)[:, 0:514].bitcast(F32), Res(), Cin[k % 2][2] if k < 2 else S.newsem("cix%d" % k)) for k in range(6)]
        CoutR = [(carve(516)[:, 0:514].bitcast(F32), Res(), s_co[k] if k < 2 else S.newsem("cox%d" % k)) for k in range(4)]
        CinbR = [(carve(258)[:, 0:257], Res()) for k in range(3)]
        ZqR = [(carve(496).rearrange("p (h c) -> p h c", c=124), Res()) for k in range(2)]
        KzR = [(carve(512, 64), Res()) for k in range(2)]
        assert off[0] <= 8192
        all_res = [r[1] for r in CinR + CoutR + CinbR + ZqR + KzR]
        fence(R_hT, all_res)
        for zq, zqr in ZqR:
            S.op("dve", I("memset", zq, 0.0), writes=[zqr])
        ptS, prS = psum()
        fns = [I("matmul", ptS[P, h * 64:(h + 1) * 64], lhsT=qk[:, 4 + h, 0:ST], rhs=qk[:, h, 0:ST], start=True, stop=True)
               for h in range(NH)]
        S.group("pe", fns, reads=qkr, writes=[prS])
        S.op("dve", I("tensor_tensor", out=SwT[P, :, :], in0=ptS[P, 0:256].rearrange("p (h t) -> p h t", t=64),
                      in1=smask[:, :].unsqueeze(1).to_broadcast([ST, NH, 64]), op=ALU.mult),
             reads=[prS, R_const], writes=[R_SwT])
        pN = [psum(reserve=True) for _ in range(NH)]
        for h in range(NH):
            pn, pnr = pN[h]
            S.group("pe", [I("matmul", pn[P, 0:DVX], lhsT=SwT[P, h, :], rhs=Vx[P, 0, h, :], start=True, stop=False)],
                    reads=[R_SwT, R_Vx[0]], writes=[pnr])
        live = {}

        def phA(idx):
            i, h = idx // NH, idx % NH
            zq, zqr = ZqR[i % 2]
            kz, kzr = KzR[i % 2]
            if h == 0:
                S.op("dve", I("tensor_copy", out=zq[:, :, 60:64], in_=qk[:, 0:NH, 4 * i:4 * i + 4]), reads=qkr + [zqr], writes=[zqr])
                S.op("dve", I("tensor_scalar", out=kz, in0=k_tm[P, 0, :], scalar1=rowmask[:, i:i + 1], scalar2=None, op0=ALU.mult),
                     reads=[R_ktm[0], R_const], writes=[kzr])
            ci, cir, cis = CinR[idx % 6]
            cb, cbr = CinbR[idx % 3]
            pn, pnr = pN[h]
            S.dma_group("sp", [I("dma_start", out=ci[:, 0:DV], in_=sC_d[i, h, :, :]),
                                I("dma_start", out=ci[:, DV:DVX], in_=sn_d[i, h, :].unsqueeze(1))], writes=[cir], dsem=cis)
            S.op("dve", I("tensor_scalar", out=ci, in0=ci, scalar1=decall[:, h, i:i + 1], scalar2=None, op0=ALU.mult),
                 reads=[cir, R_decall], writes=[cir])
            S.op("act", I("activation", out=cb, in_=ci, func=AF.Copy), reads=[cir], writes=[cbr])
            S.group("pe", [I("matmul", pn[P, 0:DVX], lhsT=zq[:, h, 60 - 4 * i:124 - 4 * i], rhs=cb, start=False, stop=(i == NS - 1))],
                    reads=[zqr, cbr, pnr], writes=[pnr])
            pc, pcr = psum()
            S.group("pe", [I("matmul", pc[:, 0:DVX], lhsT=kz[:, h * DK:(h + 1) * DK], rhs=Vx[P, 0, h, :], start=True, stop=True)],
                    reads=[kzr, R_Vx[0]], writes=[pcr])
            live[idx] = (pc, pcr)

        def phB(idx):
            i, h = idx // NH, idx % NH
            ci, cir, cis = CinR[idx % 6]
            co, cor, cos = CoutR[idx % 4]
            pc, pcr = live.pop(idx)
            S.op("dve", I("tensor_tensor", out=co, in0=pc[:, 0:DVX], in1=ci, op=ALU.add), reads=[pcr, cir], writes=[cor])
            S.op("pool", I("tensor_copy", out=nout[:, idx:idx + 1], in_=co[:, DV:DVX]), reads=[cor], writes=[R_nout])
            S.op("pool", I("dma_start", out=oC_d[i, h, :, :], in_=co[:, 0:DV]), reads=[cor], dsem=cos)
        pipeline(NS * NH, [phA, phB], lag=2)
        ptn, prn = psum()
        S.group("pe", [I("transpose", out=ptn[0:NS * NH, 0:128], in_=nout[:, :], identity=ident_f[:, :])], reads=[R_nout, R_const], writes=[prn])
        S.op("dve", I("tensor_copy", out=noutT[:, :], in_=ptn[0:NS * NH, 0:128]), reads=[prn], writes=[R_noutT])
        S.op("sp", I("dma_start", out=on_d.rearrange("i h d -> (i h) d"), in_=noutT[:, :]), reads=[R_noutT], dsem=s_st)
        for h in range(NH):
            head_out(pN[h][0], pN[h][1], 0, h, ST, otm, otr)
            psum_release(pN[h][1])
        S.op("sp", I("dma_start", out=omT_d[:, :], in_=s_mo[:, :]), reads=[R_smo], dsem=s_st)
        fence(all_res, R_hT)
        samp_sems.extend([c[2] for c in CoutR])

    def kv_stage(NB, PTK, wv, wr, last_store, pre=None):
        S.tag = S.tile + ':kv'
        P = slice(0, PTK)
        fm, fmr = pre if pre is not None else norm_T(NB, PTK, 2)

        def phA(b):
            kf, R_kf = kfR[b % 2]
            kn, R_kn = knR[b % 2]
            sc = small[P, 32 + 4 * (b % 2):36 + 4 * (b % 2)]
            pt, pr = psum()
            fns = [I("matmul", pt[P, :], lhsT=fm[:, k, b * PTK:(b + 1) * PTK], rhs=wv[:, k, 0:512], start=(k == 0), stop=(k == 7))
                   for k in range(8)]
            S.group("pe", fns, reads=[fmr[b], wr], writes=[pr])
            S.op("act", I("activation", out=kf[P, :], in_=pt[P, :], func=AF.Copy), reads=[pr], writes=[R_kf])
            S.op("dve", I("tensor_tensor", out=kn[P, :], in0=kf[P, 0:256], in1=kf[P, 0:256], op=ALU.mult), reads=[R_kf], writes=[R_kn])
            S.op("dve", I("tensor_reduce", out=sc, in_=kn[P, :].rearrange("p (g d) -> p g d", d=64), axis=AX.X, op=ALU.add),
                 reads=[R_kn], writes=[R_smk[b % 2]])

        def phB(b):
            kf, R_kf = kfR[b % 2]
            kn, R_kn = knR[b % 2]
            Ktm, R_Ktm = KtmR[b % 2]
            sc = small[P, 32 + 4 * (b % 2):36 + 4 * (b % 2)]
            rsqrt_small(sc, sc, 4, 1.0 / 64, [R_smk[b % 2]])
            S.op("dve", I("tensor_tensor", out=kn[P, :].rearrange("p (g d) -> p g d", d=64),
                          in0=kf[P, 0:256].rearrange("p (g d) -> p g d", d=64),
                          in1=sc.unsqueeze(2).to_broadcast([PTK, 4, 64]), op=ALU.mult),
                 reads=[R_kf, R_smk[b % 2], R_kn], writes=[R_kn])
            S.op("dve", I("tensor_tensor", out=kn[P, :].rearrange("p (g d) -> p g d", d=64),
                          in0=kn[P, :].rearrange("p (g d) -> p g d", d=64),
                          in1=gk_b[P, :].unsqueeze(1).to_broadcast([PTK, 4, 64]), op=ALU.mult),
                 reads=[R_kn, R_const], writes=[R_kn])
            S.op("act", I("activation", out=Ktm[P, :], in_=kn[P, :], func=AF.Copy), reads=[R_kn], writes=[R_Ktm])
            S.op("act", I("activation", out=Vex[P, b + 1, :, 0:64], in_=kf[P, 256:512].rearrange("p (g d) -> p g d", d=64), func=AF.Copy),
                 reads=[R_kf], writes=[R_Vex[b + 1]])
            if last_store is not None and b == NB - 1:
                last_store(kn, R_kn, kf, R_kf)
            ptt, prt = psum()
            ptb = ptt.bitcast(BF16)
            fns = [I("transpose", out=ptb[:, gp * PTK:(gp + 1) * PTK], in_=Ktm[P, gp * 128:(gp + 1) * 128], identity=ident_b[P, P])
                   for gp in range(2)]
            S.group("pe", fns, reads=[R_Ktm, R_const], writes=[prt])
            S.op("dve", I("tensor_copy", out=kT[:, :, (b + 1) * 128:(b + 1) * 128 + PTK],
                          in_=ptb[:, 0:2 * PTK].rearrange("p (g t) -> p g t", t=PTK)), reads=[prt], writes=[R_kT[b + 1]])
        pipeline(NB, [phA, phB], lag=1)

    def carry_prev(NB, use_flag):
        S.op("dve", I("tensor_copy", out=kT[:, :, 0:128], in_=kT[:, :, NB * 128:(NB + 1) * 128]),
             reads=[R_kT[NB], R_kT[0]], writes=[R_kT[0]])
        if use_flag:
            S.op("dve", I("tensor_scalar", out=Vex[:, 0, :, :], in0=Vex[:, NB, :, :], scalar1=flag[:, 0:1], scalar2=None,
                                                  op0=ALU.mult), reads=[R_Vex[NB], R_const, R_Vex[0]], writes=[R_Vex[0]])
        else:
            S.op("dve", I("tensor_copy", out=Vex[:, 0, :, :], in_=Vex[:, NB, :, :]), reads=[R_Vex[NB], R_Vex[0]], writes=[R_Vex[0]])

    def q_stage(NB, PTK, wv, wr, pre=None):
        S.tag = S.tile + ':q'
        P = slice(0, PTK)
        fm, fmr = pre if pre is not None else norm_T(NB, PTK, 3)
        qtm, qtr = next_tm()
        qT, qTr = next_fm()

        def phA(u):
            b, gp2 = u // 2, u % 2
            qf, R_qf = qfR[u % 2]
            sq, R_sq = kfR[u % 2]
            sc = small[P, 40 + 8 * (u % 2):48 + 8 * (u % 2)]
            pt, pr = psum()
            fns = [I("matmul", pt[P, :], lhsT=fm[:, k, b * PTK:(b + 1) * PTK],
                     rhs=wv[:, k, gp2 * 512:(gp2 + 1) * 512], start=(k == 0), stop=(k == 7)) for k in range(8)]
            S.group("pe", fns, reads=[fmr[b], wr], writes=[pr])
            S.op("act", I("activation", out=qf[P, :], in_=pt[P, :], func=AF.Copy), reads=[pr], writes=[R_qf])
            S.op("dve", I("tensor_tensor", out=sq[P, :], in0=qf[P, :], in1=qf[P, :], op=ALU.mult), reads=[R_qf], writes=[R_sq])
            S.op("dve", I("tensor_reduce", out=sc, in_=sq[P, :].rearrange("p (h d) -> p h d", d=64), axis=AX.X, op=ALU.add),
                 reads=[R_sq], writes=[R_smq[u % 2]])

        def phB(u):
            b, gp2 = u // 2, u % 2
            qf, R_qf = qfR[u % 2]
            sc = small[P, 40 + 8 * (u % 2):48 + 8 * (u % 2)]
            rsqrt_small(sc, sc, 8, 1.0 / 64, [R_smq[u % 2]])
            S.op("dve", I("tensor_tensor", out=qf[P, :].rearrange("p (h d) -> p h d", d=64),
                          in0=qf[P, :].rearrange("p (h d) -> p h d", d=64),
                          in1=gq_b[P, :].unsqueeze(1).to_broadcast([PTK, 8, 64]), op=ALU.mult),
                 reads=[R_qf, R_const], writes=[R_qf])
            src = qf[P, :].rearrange("p (e j d) -> p e j d", e=2, j=4)
            rs = sc.rearrange("p (e j) -> p e j", e=2).unsqueeze(3).to_broadcast([PTK, 2, 4, 64])
            dst = qtm[P, b, gp2 * 512:(gp2 + 1) * 512].rearrange("p (j e d) -> p e j d", e=2, j=4)
            S.op("dve", I("tensor_tensor", out=dst, in0=src, in1=rs, op=ALU.mult), reads=[R_qf, R_smq[u % 2]], writes=[qtr[b]])
            if gp2 == 1:
                tm_to_fm_blk(qtm, qtr, qT, qTr, b, PTK, None, use_act=(b % 2 == 0))
        pipeline(NB * 2, [phA, phB], lag=1)
        return qT, qTr

    def attn_prompt(NB, qT, qTr, wv_o, wr_o_get):
        S.tag = S.tile + ':attn'
        otm, otr = next_tm()

        def phA(u):
            b, g = u // 4, u % 4
            base = (g % 2) * 64
            pb = slice(base, base + 64)
            P0 = (g // 2) * 4
            ptile, ptr_ = PT[u % 2]
            for kb in range(2):
                slot = b + kb
                pt, pr = psum()
                S.group("pe", [I("matmul", pt[:, :], lhsT=kT[pb, g // 2, slot * 128:(slot + 1) * 128],
                                 rhs=qT[pb, P0:P0 + 4, b * 128:(b + 1) * 128], start=True, stop=True)],
                        reads=[R_kT[slot], qTr[b]], writes=[pr])
                S.op("act", I("activation", out=ptile[:, kb, :], in_=pt[:, :], func=AF.Exp, scale=0.125), reads=[pr], writes=[ptr_])
                S.op("dve", I("tensor_tensor", out=ptile[:, kb, :].rearrange("p (j q) -> p j q", q=128),
                              in0=ptile[:, kb, :].rearrange("p (j q) -> p j q", q=128),
                              in1=amask[:, kb * 128:(kb + 1) * 128].unsqueeze(1).to_broadcast([128, 4, 128]), op=ALU.mult),
                     reads=[ptr_, R_const], writes=[ptr_])

        def phB(u):
            b, g = u // 4, u % 4
            ptile, ptr_ = PT[u % 2]
            po, por = psum()
            fns = []
            for j in range(4):
                for kb in range(2):
                    fns.append(I("matmul", po[:, j * 65:(j + 1) * 65], lhsT=ptile[:, kb, j * 128:(j + 1) * 128],
                                 rhs=Vex[:, b + kb, g, :], start=(kb == 0), stop=(kb == 1)))
            S.group("pe", fns, reads=[ptr_, R_Vex[b], R_Vex[b + 1]], writes=[por])
            attn_finish(po, por, 128, b, g, otm, otr)
        pipeline(NB * 4, [phA, phB], lag=1)
        fo, fo_r = next_fm()
        tm_to_fm(otm, otr, fo, fo_r, NB, 128, None)
        return fo, fo_r

    def attn_finish(po, por, PTK, b, g, otm, otr):
        P = slice(0, PTK)
        po3 = po[P, 0:260].rearrange("p (j d) -> p j d", d=65)
        S.op("dve", I("tensor_tensor", out=small[P, 56:60], in0=po3[:, :, 64:65].rearrange("p j d -> p (j d)"),
                                              in1=sink_b[P, g * 4:(g + 1) * 4], op=ALU.add),
             reads=[por, R_const], writes=[R_sma])
        S.op("dve", I("reciprocal", out=small[P, 56:60], in_=small[P, 56:60]), reads=[R_sma], writes=[R_sma])
        S.op("dve", I("tensor_tensor", out=otm[P, b, g * 256:(g + 1) * 256].rearrange("p (j d) -> p j d", d=64),
                                              in0=po3[:, :, 0:64], in1=small[P, 56:60].unsqueeze(2).to_broadcast([PTK, 4, 64]), op=ALU.mult),
             reads=[por, R_sma], writes=[otr[b]])

    def attn_sample(qT, qTr):
        S.tag = S.tile + ':attn'
        P = slice(0, ST)
        otm, otr = next_tm()
        cvb = hT[:, :, :].rearrange("p a b -> p (a b)")[:, 0:4160].rearrange("p (i g d) -> p i g d", g=4, d=65)
        fence(R_hT, [R_cvb])
        S.op("dve", I("memset", cvb, 1.0), writes=[R_cvb])
        ckb = xres[:, 1:3, :].bitcast(BF16).rearrange("p a (i c) -> p (a i) c", c=256)
        R_ckb = [R_x[1], R_x[2]]
        S.op("pool", I("dma_start", out=ckb, in_=cwk_d.rearrange("i w c -> w i c")), writes=R_ckb, dsem=s_ck)
        S.dma_group("pool", [I("dma_start", out=cvb[:, :, g_, 0:64], in_=cwv_d[:, :, g_ * 64:(g_ + 1) * 64].rearrange("i w d -> w i d"))
                             for g_ in range(4)], reads=[R_cvb], writes=[R_cvb], dsem=s_cv)
        for g in range(4):
            base = (g % 2) * 64
            pb = slice(base, base + 64)
            P0 = (g // 2) * 4
            if g % 2 == 0:
                for i in range(NS):
                    ptt, prt = psum()
                    ptb = ptt.bitcast(BF16)
                    S.group("pe", [I("transpose", out=ptb[:, 0:128], in_=ckb[:, i, (g // 2) * 128:(g // 2 + 1) * 128], identity=ident_b[:, :])],
                            reads=R_ckb + [R_const], writes=[prt])
                    evac(i, KcT[:, i, :], ptb[:, 0:128], [prt], [R_KcT])
            pt, pr = psum()
            S.group("pe", [I("matmul", pt[P, 0:256], lhsT=kT[pb, g // 2, 128:128 + ST], rhs=qT[pb, P0:P0 + 4, 0:ST],
                                              start=True, stop=True)],
                    reads=[R_kT[1], qTr[0]], writes=[pr])
            S.op("act", I("activation", out=PTn[:, :, :].rearrange("p j q -> p (j q)"), in_=pt[P, 0:256], func=AF.Exp, scale=0.125),
                 reads=[pr], writes=[R_PTn])
            S.op("dve", I("tensor_tensor", out=PTn[:, :, :], in0=PTn[:, :, :], in1=smask[:, :].unsqueeze(1).to_broadcast([ST, 4, ST]),
                                                  op=ALU.mult), reads=[R_PTn, R_const], writes=[R_PTn])
            for i in range(NS):
                pt2, pr2 = psum()
                S.group("pe", [I("matmul", pt2[:, 0:16], lhsT=KcT[pb, i, :], rhs=qT[pb, P0:P0 + 4, 4 * i:4 * i + 4],
                                                                  start=True, stop=True)],
                        reads=[R_KcT, qTr[0]], writes=[pr2])
                S.op("act", I("activation", out=PTc[:, i, :, 4 * i:4 * i + 4],
                                                                   in_=pt2[:, 0:16].rearrange("p (j t) -> p j t", t=4), func=AF.Exp, scale=0.125),
                     reads=[pr2], writes=[R_PTc[i]])
                S.op("dve", I("tensor_tensor", out=PTc[:, i, :, 4 * i:4 * i + 4], in0=PTc[:, i, :, 4 * i:4 * i + 4],
                                                                 in1=cmask[:, :].unsqueeze(1).to_broadcast([128, 4, 4]), op=ALU.mult),
                     reads=[R_PTc[i], R_const], writes=[R_PTc[i]])
            po, por = psum()
            fns = []
            for j in range(4):
                for i in range(NS):
                    fns.append(I("matmul", po[P, j * 65:(j + 1) * 65], lhsT=PTc[:, i, j, :], rhs=cvb[:, i, g, :],
                                                                start=(i == 0), stop=False))
                fns.append(I("matmul", po[P, j * 65:(j + 1) * 65], lhsT=PTn[:, j, :], rhs=Vex[P, 1, g, :],
                                                         start=False, stop=True))
            S.group("pe", fns, reads=R_PTc + [R_PTn, R_cvb, R_Vex[1]], writes=[por])
            attn_finish(po, por, ST, 0, g, otm, otr)
        fo, fo_r = next_fm()
        tm_to_fm(otm, otr, fo, fo_r, 1, ST, None)
        fence([R_cvb], R_hT)
        return fo, fo_r

    def load_x(src_rows, NB, PTK):
        for b in range(NB):
            S.op("sp", I("dma_start", out=xres[0:PTK, b, :], in_=src_rows[b * PTK:(b + 1) * PTK, :]),
                 writes=[R_x[b]], dsem=s_ldx[b])

    def plan_tile(kind, mode="f32"):
        mlp_chunks = ("w1_0", "w1_1", "w2_0", "w1_2", "w2_1", "w1_3", "w2_2", "w2_3")
        if kind == "warm":
            for k in ("inB", "inQK", "inV", "out") + mlp_chunks:
                plan_chunk(k, 0, mode)
            plan_chunk("kv", 0, mode)
        else:
            for k in ("inB", "inQK", "inV", "out") + mlp_chunks:
                plan_chunk(k, 0, mode)
            plan_chunk("kv", 0, mode); plan_chunk("q", 0, mode); plan_chunk("o", 0, mode)
            for k in mlp_chunks:
                plan_chunk(k, 1, mode)

    pre_tiles = [(0, 4), (512, 4), (1024, 4), (1536, 3)]
    for k_ in range(len(pre_tiles)):
        plan_chunk("inB", 0, "part"); plan_chunk("inQK", 0, "part")
        if k_ >= 1:
            plan_chunk("inV", 0, "part")
    plan_chunk("inV", 0, "part")
    plan_tile("warm")
    for t_ in range(4):
        plan_tile("main", "f32wb" if t_ == 3 else "f32")
    plan_tile("main", "sc")

    pipeline(len(pre_tiles), [lambda k: prefix_A(k, *pre_tiles[k]), lambda k: prefix_B(k, *pre_tiles[k])], lag=1)
    sel(0)
    S.tile = 'warm'
    load_x(xpre[1920:2048, :], 1, 128)
    mlstm(1, 128, True, False)
    mlp(1, 128, 0)
    wv, wr = wget("kv")
    kv_stage(1, 128, wv, wr, None)
    wdone()
    carry_prev(1, True)
    S.op("dve", I("tensor_scalar", out=Cst[:, :, :], in0=Cst[:, :, :], scalar1=flag[:, 0:1], scalar2=None, op0=ALU.mult),
         reads=[R_Cst, R_const], writes=[R_Cst])
    S.op("dve", I("tensor_scalar", out=carryA[:, :], in0=carryA[:, :], scalar1=flag[0:4, 0:1], scalar2=None, op0=ALU.mult),
         reads=[R_cA, R_const], writes=[R_cA])
    S.op("dve", I("tensor_scalar", out=carryM[:, :], in0=carryM[:, :], scalar1=flag[0:4, 0:1], scalar2=None, op0=ALU.mult),
         reads=[R_cM, R_const], writes=[R_cM])

    for t in range(4):
        r0 = t * 512
        S.tile = 'main%d' % t
        selx(t % 2)
        if t == 0:
            load_x(xp[r0:r0 + 512, :], 4, 128)
            ml_pre = None
        mlstm(4, 128, True, False, pre=ml_pre)
        mlp(4, 128, 0)
        S.tag = S.tile + ':kv'
        fkv, fkvr, fq, fqr = norm_T2(4, 128, 2, 3)
        wv, wr = wget("kv")
        ls = None
        if t == 3:
            def ls(kn_, rkn, kf_, rkf):
                S.op("sp", I("dma_start", out=pwk_d[:, :], in_=kn_[:, :]), reads=[rkn], dsem=s_st)
                S.op("sp", I("dma_start", out=pwv_d[:, :], in_=kf_[:, 256:512]), reads=[rkf], dsem=s_st)
        kv_stage(4, 128, wv, wr, ls, pre=(fkv, fkvr))
        wdone()
        wv, wr = wget("q")
        qT, qTr = q_stage(4, 128, wv, wr, pre=(fq, fqr))
        wdone()
        fo, fo_r = attn_prompt(4, qT, qTr, None, None)
        carry_prev(4, False)
        S.tag = S.tile + ':wo'
        wv, wr = wget("o")
        resid_add(4, 128, fo, fo_r, wv, wr, 8)
        wdone()
        if t < 3:
            selx((t + 1) % 2)
            load_x(xp[r0 + 512:r0 + 1024, :], 4, 128)
            selx(t % 2)
            nxt = {}

            def hook(t=t):
                tag = (S.tile, S.tag)
                selx((t + 1) % 2)
                S.tile = 'main%d' % (t + 1)
                nxt['pre'] = norm_T(4, 128, 0)
                selx(t % 2)
                S.tile, S.tag = tag
            mlp(4, 128, 1, hook=hook)
            ml_pre = nxt['pre']
        else:
            mlp(4, 128, 1)
        for b in range(4):
            S.op("sp", I("dma_start", out=y_d[r0 + b * 128:r0 + (b + 1) * 128, :], in_=xres[:, b, :]),
                 reads=[R_x[b]], dsem=s_sty[b])
    for h in range(NH):
        S.op("sp", I("dma_start", out=pC_d[h, :, :], in_=Cst[:, h, 0:DV]), reads=[R_Cst], dsem=s_st)
        S.op("sp", I("dma_start", out=pn_d[h, :].unsqueeze(1), in_=Cst[:, h, DV:DVX]), reads=[R_Cst], dsem=s_st)
    mfin = grow["t2"][0]
    S.op("dve", I("tensor_tensor", out=mfin[:, 0:1], in0=carryM[:, 0:1], in1=carryA[:, 0:1], op=ALU.subtract),
         reads=[R_cM, R_cA, grow["t2"][1]], writes=[grow["t2"][1]])
    S.op("sp", I("dma_start", out=pm_d[:, :], in_=mfin[:, 0:1]), reads=[grow["t2"][1]], dsem=s_st)

    S.tile = 'samp'
    selx(0)
    load_x(xs_d, 1, ST)
    mlstm(1, ST, True, True)
    mlp(1, ST, 0)
    wv, wr = wget("kv")

    def ls_s(kn_, rkn, kf_, rkf):
        for i in range(NS):
            S.op("sp", I("dma_start", out=owk_d[i, 124:128, :], in_=kn_[4 * i:4 * i + 4, :]), reads=[rkn], dsem=s_st)
            S.op("sp", I("dma_start", out=owv_d[i, 124:128, :], in_=kf_[4 * i:4 * i + 4, 256:512]), reads=[rkf], dsem=s_st)
    kv_stage(1, ST, wv, wr, ls_s)
    wdone()
    wv, wr = wget("q")
    qT, qTr = q_stage(1, ST, wv, wr)
    wdone()
    fo, fo_r = attn_sample(qT, qTr)
    wv, wr = wget("o")
    resid_add(1, ST, fo, fo_r, wv, wr, 8)
    wdone()
    mlp(1, ST, 1)
    S.op("sp", I("dma_start", out=ys_d[:, :], in_=xres[0:ST, 0, :]), reads=[R_x[0]], dsem=s_sty[0])
    S.op("sp", I("dma_start", out=owk_d[:, 0:124, :], in_=cwk_d[:, 4:128, :]), dsem=s_st)
    S.op("sp", I("dma_start", out=owv_d[:, 0:124, :], in_=cwv_d[:, 4:128, :]), dsem=s_st)

    _NC_CACHE['sbuf_free'] = nc.sbuf_bytes_remaining
    final_waits = [(k, S.cnt[k]) for k in list(dict.fromkeys([s_st] + s_sty2[0] + s_sty2[1] + s_co + samp_sems)) if S.cnt[k] > 0]
    with nc.Block() as block:
        S.emit(block, final_waits)
    es.close()
    return nc


_NC_CACHE = {}


def _consts():
    c = {}
    c["ident_b"] = np.eye(128, dtype=np.float32).astype(ml_dtypes.bfloat16)
    c["ident_f"] = np.eye(128, dtype=np.float32)
    p = np.arange(128)[:, None]
    t = np.arange(64)[None, :]
    c["mlmask"] = ((p % 64) <= t).astype(np.float32).astype(ml_dtypes.bfloat16)
    ps = np.arange(64)[:, None]
    c["smask"] = (((ps // 4) == (t // 4)) & (ps <= t)).astype(np.float32).astype(ml_dtypes.bfloat16)
    q = np.arange(128)[None, :]
    am = np.concatenate([(p >= q), (p <= q)], axis=1)
    c["amask"] = am.astype(np.float32).astype(ml_dtypes.bfloat16)
    c["cmask"] = (p >= np.arange(4)[None, :]).astype(np.float32).astype(ml_dtypes.bfloat16)
    c["rowmask"] = ((ps // 4) == np.arange(16)[None, :]).astype(np.float32)
    c["eye4"] = np.eye(4, dtype=np.float32)
    c["ones4"] = np.ones((4, 128), np.float32)
    return c


def kernel(x_prompt, x_sample, state_mlstm_C, state_mlstm_n, state_mlstm_m, cache_win_k, cache_win_v,
           ml_norm_g, ml_w_in, ml_b_i, ml_b_f, ml_head_g, ml_w_out,
           kv_norm_g, w_kv, k_norm_g,
           att_norm_g, att_w_q, q_norm_g, att_sinks, att_w_o,
           mlp_norm_g, mlp_w1, mlp_w2):
    f = lambda a: np.ascontiguousarray(np.asarray(a, dtype=np.float32))
    x_prompt = f(x_prompt); x_sample = f(x_sample)
    sC = f(state_mlstm_C); sn = f(state_mlstm_n); sm = f(state_mlstm_m)
    cwk = f(cache_win_k); cwv = f(cache_win_v)
    if "nc" not in _NC_CACHE:
        _NC_CACHE["nc"] = build_program()
    nc = _NC_CACHE["nc"]

    def col(g):
        return f(g).reshape(8, 128).T
    gcols = np.ascontiguousarray(np.concatenate(
        [col(ml_norm_g[0]), col(mlp_norm_g[0]), col(kv_norm_g), col(att_norm_g[0]), col(mlp_norm_g[1]), col(ml_head_g[0])], axis=1))
    shared = dict(
        w_in=f(ml_w_in[0]), w_out=f(ml_w_out[0]), w1=f(mlp_w1), w2=f(mlp_w2), w_kv=f(w_kv), w_q=f(att_w_q[0]), w_o=f(att_w_o[0]),
        gcols=gcols, bif=np.ascontiguousarray(np.stack([f(ml_b_i[0]), f(ml_b_f[0])], axis=1)),
        gk_b=np.ascontiguousarray(np.broadcast_to(f(k_norm_g)[None, :], (128, 64))),
        gq_b=np.ascontiguousarray(np.broadcast_to(f(q_norm_g[0])[None, :], (128, 64))),
        sink_b=np.ascontiguousarray(np.broadcast_to(f(att_sinks[0])[None, :], (128, 16))),
    )
    shared.update(_consts())
    in_maps = []
    for c in range(8):
        b, hh = c // 2, c % 2
        m = dict(shared)
        m["xp"] = np.ascontiguousarray(x_prompt[b, hh * HALF:(hh + 1) * HALF])
        m["xpre"] = np.ascontiguousarray(x_prompt[b, 0:HALF]) if hh == 1 else np.zeros((HALF, D), np.float32)
        m["flag"] = np.full((128, 1), float(hh), np.float32)
        sl = slice(c * NS, (c + 1) * NS)
        m["xs"] = np.ascontiguousarray(x_sample[sl].reshape(ST, D))
        m["sC"] = np.ascontiguousarray(sC[0, sl])
        m["sn"] = np.ascontiguousarray(sn[0, sl])
        m["smT"] = np.ascontiguousarray(sm[0, sl].T)
        m["cwk"] = np.ascontiguousarray(cwk[sl].reshape(NS, 128, 256))
        m["cwv"] = np.ascontiguousarray(cwv[sl].reshape(NS, 128, 256))
        in_maps.append(m)
    res = run_bass_kernel_spmd(nc, in_maps, core_ids=list(range(8)))
    R = res.results
    B = 4
    y_prompt = np.zeros((B, 4096, D), np.float32)
    y_sample = np.zeros((128, 4, D), np.float32)
    p_C = np.zeros((1, B, NH, DK, DV), np.float32); p_n = np.zeros((1, B, NH, DK), np.float32); p_m = np.zeros((1, B, NH), np.float32)
    p_wk = np.zeros((B, 128, 4, 64), np.float32); p_wv = np.zeros((B, 128, 4, 64), np.float32)
    s_C = np.zeros((1, 128, NH, DK, DV), np.float32); s_n = np.zeros((1, 128, NH, DK), np.float32); s_m = np.zeros((1, 128, NH), np.float32)
    s_wk = np.zeros((128, 128, 4, 64), np.float32); s_wv = np.zeros((128, 128, 4, 64), np.float32)
    for c in range(8):
        b, hh = c // 2, c % 2
        r = R[c]
        y_prompt[b, hh * HALF:(hh + 1) * HALF] = r["y"]
        sl = slice(c * NS, (c + 1) * NS)
        y_sample[sl] = r["ys"].reshape(NS, 4, D)
        if hh == 1:
            p_C[0, b] = r["pC"]; p_n[0, b] = r["pn"]; p_m[0, b] = r["pm"].reshape(NH)
            p_wk[b] = r["pwk"].reshape(128, 4, 64); p_wv[b] = r["pwv"].reshape(128, 4, 64)
        s_C[0, sl] = r["oC"]; s_n[0, sl] = r["on"]; s_m[0, sl] = r["omT"].T
        s_wk[sl] = r["owk"].reshape(NS, 128, 4, 64); s_wv[sl] = r["owv"].reshape(NS, 128, 4, 64)
    return (y_prompt, y_sample, p_C, p_n, p_m, p_wk, p_wv, s_C, s_n, s_m, s_wk, s_wv)
```
